# Optimizing a Trainium2 kernel written in Bass

```python
import jax, jax.numpy as jnp
from jax import lax
import numpy as np

D_MODEL = 1024
BATCH = 8
SEQ = 2048
DEPTH = 1
DEC_BATCH = 8
DEC_SEQ = 16
PAST_LEN = 1024

CHUNK = 64
Q_BLOCK = 128
D_MIX = D_MODEL
SB_WIDTH = D_MIX // 2
SB_HEAD_DIM = 64
SB_HEADS = SB_WIDTH // SB_HEAD_DIM
HG_WIDTH = D_MIX - SB_WIDTH
HG_HEAD_DIM = 128
HG_HEADS = HG_WIDTH // HG_HEAD_DIM
SPLIT_WIDTHS = [SB_WIDTH] * 4 + [HG_WIDTH] * 4
D_IN = sum(SPLIT_WIDTHS)
DEEPNORM_ALPHA = (2 * DEPTH) ** 0.25
DEEPNORM_BETA = (8 * DEPTH) ** -0.25
LN_EPS = 1e-5
RMS_EPS = 1e-6

kernel_name = "stickbreak_hgrn2_deepnorm_stream_step"


def layer_norm(x, g, b):
    x = x.astype(jnp.float32)
    mu = jnp.mean(x, axis=-1, keepdims=True)
    var = jnp.mean(jnp.square(x - mu), axis=-1, keepdims=True)
    return (x - mu) * lax.rsqrt(var + LN_EPS) * g.astype(jnp.float32) + b.astype(jnp.float32)


def stick_breaking_attention(q, k, v, q_offset):
    B, H, Tq, dh = q.shape
    Tk = k.shape[2]
    blk = min(Q_BLOCK, Tq)
    nb = Tq // blk
    scale = dh ** -0.5
    kf = k.astype(jnp.float32)
    vf = v.astype(jnp.float32)
    kpos = jnp.arange(Tk)
    qb = q.astype(jnp.float32).reshape(B, H, nb, blk, dh).transpose(2, 0, 1, 3, 4)
    starts = q_offset + blk * jnp.arange(nb)

    def one_block(args):
        qblk, start = args
        qpos = start + jnp.arange(blk)
        z = jnp.einsum('bhqd,bhkd->bhqk', qblk, kf) * scale
        mask = kpos[None, :] < qpos[:, None]
        log_keep = jnp.where(mask, jax.nn.log_sigmoid(-z), 0.0)
        later = lax.cumsum(log_keep, axis=3, reverse=True) - log_keep
        w = jnp.where(mask, jnp.exp(jax.nn.log_sigmoid(z) + later), 0.0)
        return jnp.einsum('bhqk,bhkd->bhqd', w, vf)

    o = lax.map(one_block, (qb, starts))
    return o.transpose(1, 2, 0, 3, 4).reshape(B, H, Tq, dh)


def hgrn2_recurrence(q, k, g, i, s0):
    B, H, T, dk = q.shape
    dv = i.shape[-1]
    c = min(CHUNK, T)
    n = T // c

    def to_chunks(a):
        return a.reshape(B, H, n, c, a.shape[-1]).transpose(2, 0, 1, 3, 4)

    causal = jnp.tril(jnp.ones((c, c), dtype=bool))[:, :, None]

    def step(S, xs):
        qc, kc, gc, ic = xs
        b = jnp.cumsum(gc, axis=2)
        diff = b[:, :, :, None, :] - b[:, :, None, :, :]
        decay = jnp.where(causal, jnp.exp(jnp.minimum(diff, 0.0)), 0.0)
        scores = jnp.einsum('bhtd,bhsd,bhtsd->bhts', qc, kc, decay)
        o = (jnp.einsum('bhts,bhsv->bhtv', scores, ic)
             + jnp.einsum('bhtd,bhdv->bhtv', qc * jnp.exp(b), S))
        b_last = b[:, :, -1:, :]
        S_new = (jnp.exp(b_last)[:, :, 0, :, None] * S
                 + jnp.einsum('bhsd,bhsv->bhdv', kc * jnp.exp(b_last - b), ic))
        return S_new, o

    S_fin, o = lax.scan(step, s0, (to_chunks(q), to_chunks(k), to_chunks(g), to_chunks(i)))
    return o.transpose(1, 2, 0, 3, 4).reshape(B, H, T, dv), S_fin


def encoder_layer(x, past_k, past_v, s0, w_in, w_out, lb, norm_g, ln_g, ln_b, q_offset):
    Bn, T, _ = x.shape
    proj = jnp.einsum('btd,de->bte', x, w_in)
    idx = [int(v) for v in np.cumsum(SPLIT_WIDTHS)[:-1]]
    qa, ka, va, ga, qh, fh, ih, gh = jnp.split(proj, idx, axis=-1)

    def heads(a, h):
        return a.reshape(Bn, T, h, -1).transpose(0, 2, 1, 3)

    qa, ka, va = heads(qa, SB_HEADS), heads(ka, SB_HEADS), heads(va, SB_HEADS)
    if past_k is None:
        k_all, v_all = ka, va
    else:
        k_all = jnp.concatenate([past_k.astype(ka.dtype), ka], axis=2)
        v_all = jnp.concatenate([past_v.astype(va.dtype), va], axis=2)
    o_sb = stick_breaking_attention(qa, k_all, v_all, q_offset)
    o_sb = o_sb.transpose(0, 2, 1, 3).reshape(Bn, T, SB_WIDTH) * jax.nn.silu(ga.astype(jnp.float32))

    f = lb + (1.0 - lb) * jax.nn.sigmoid(fh.astype(jnp.float32))
    g_log = jnp.log(f)
    k_in = 1.0 - f
    q_h = jax.nn.silu(qh.astype(jnp.float32))
    o_hg, s_new = hgrn2_recurrence(heads(q_h, HG_HEADS), heads(k_in, HG_HEADS),
                                   heads(g_log, HG_HEADS), heads(ih.astype(jnp.float32), HG_HEADS),
                                   s0.astype(jnp.float32))
    o_hg = o_hg * lax.rsqrt(jnp.mean(jnp.square(o_hg), axis=-1, keepdims=True) + RMS_EPS)
    o_hg = (o_hg.transpose(0, 2, 1, 3).reshape(Bn, T, HG_WIDTH) * norm_g.astype(jnp.float32)
            * jax.nn.silu(gh.astype(jnp.float32)))

    mixed = jnp.concatenate([o_sb, o_hg], axis=-1).astype(x.dtype)
    out = jnp.einsum('bte,ed->btd', mixed, w_out)
    y = layer_norm(DEEPNORM_ALPHA * x.astype(jnp.float32) + out.astype(jnp.float32), ln_g, ln_b)
    return y.astype(x.dtype), ka, va, s_new


def setup_inputs(seed: int = 0) -> dict:
    key = jax.random.key(seed)
    ks = jax.random.split(key, 12)
    f32 = jnp.float32
    x_prompt = jax.random.normal(ks[0], (BATCH, SEQ, D_MODEL), f32)
    x_sample = jax.random.normal(ks[1], (DEC_BATCH, DEC_SEQ, D_MODEL), f32)
    cache_k = jax.random.normal(ks[2], (DEPTH, DEC_BATCH, SB_HEADS, PAST_LEN, SB_HEAD_DIM), f32)
    cache_v = DEEPNORM_BETA * jax.random.normal(ks[3], (DEPTH, DEC_BATCH, SB_HEADS, PAST_LEN, SB_HEAD_DIM), f32)
    state_s = 0.5 * jax.random.normal(ks[4], (DEPTH, DEC_BATCH, HG_HEADS, HG_HEAD_DIM, HG_HEAD_DIM), f32)
    col_scale = jnp.concatenate([
        jnp.ones((SB_WIDTH * 2,), f32), jnp.full((SB_WIDTH,), DEEPNORM_BETA, f32), jnp.ones((SB_WIDTH,), f32),
        jnp.ones((HG_WIDTH * 2,), f32), jnp.full((HG_WIDTH,), DEEPNORM_BETA, f32), jnp.ones((HG_WIDTH,), f32)])
    w_in = jax.random.normal(ks[5], (DEPTH, D_MODEL, D_IN), f32) * (D_MODEL ** -0.5) * col_scale
    w_out = jax.random.normal(ks[6], (DEPTH, D_MIX, D_MODEL), f32) * (D_MIX ** -0.5) * DEEPNORM_BETA
    lb_logits = 0.1 * jax.random.normal(ks[7], (DEPTH + 1, HG_WIDTH), f32)
    hgrn_norm_g = 1.0 + 0.01 * jax.random.normal(ks[8], (DEPTH, HG_WIDTH), f32)
    ln_g = 1.0 + 0.01 * jax.random.normal(ks[9], (DEPTH, D_MODEL), f32)
    ln_b = 0.01 * jax.random.normal(ks[10], (DEPTH, D_MODEL), f32)
    return {"x_prompt": x_prompt, "x_sample": x_sample, "cache_k": cache_k, "cache_v": cache_v,
            "state_s": state_s, "w_in": w_in, "w_out": w_out, "lb_logits": lb_logits,
            "hgrn_norm_g": hgrn_norm_g, "ln_g": ln_g, "ln_b": ln_b}


def reference(x_prompt, x_sample, cache_k, cache_v, state_s, w_in, w_out, lb_logits,
              hgrn_norm_g, ln_g, ln_b):
    lower_bounds = jnp.cumsum(jax.nn.softmax(lb_logits.astype(jnp.float32), axis=0), axis=0)
    past_len = cache_k.shape[3]
    yp, ys = x_prompt, x_sample
    kp_l, vp_l, sp_l, ks_l, vs_l, ss_l = [], [], [], [], [], []
    for l in range(DEPTH):
        s0p = jnp.zeros((x_prompt.shape[0], HG_HEADS, HG_HEAD_DIM, HG_HEAD_DIM), jnp.float32)
        yp, kp, vp, sp = encoder_layer(yp, None, None, s0p, w_in[l], w_out[l], lower_bounds[l],
                                       hgrn_norm_g[l], ln_g[l], ln_b[l], 0)
        ys, kn, vn, sn = encoder_layer(ys, cache_k[l], cache_v[l], state_s[l], w_in[l], w_out[l],
                                       lower_bounds[l], hgrn_norm_g[l], ln_g[l], ln_b[l], past_len)
        kp_l.append(kp); vp_l.append(vp); sp_l.append(sp)
        ks_l.append(kn); vs_l.append(vn); ss_l.append(sn)
    return (yp, ys, jnp.stack(kp_l), jnp.stack(vp_l), jnp.stack(sp_l),
            jnp.stack(ks_l), jnp.stack(vs_l), jnp.stack(ss_l))
```

```python
import contextlib
import os
import numpy as np
import ml_dtypes
import concourse.bass as bass
import concourse.mybir as mybir
from concourse.bass_utils import run_bass_kernel_spmd

F32 = mybir.dt.float32
BF16 = mybir.dt.bfloat16
ALU = mybir.AluOpType
AF = mybir.ActivationFunctionType

NCORES = 8
D = 1024
T = 2048
TS = 16
TT = T + TS
PAST = 1024
ALPHA = 2.0 ** 0.25
LN_EPS = 1e-5
RMS_EPS = 1e-6
NJUNK = 0
STRICT_SYNC = os.environ.get("KSTRICT", "1") == "1"
ENGS = ("pe", "act", "dve", "pool", "sp")


class Buf:
    __slots__ = ("name", "last_w", "rd_eng", "rd_dma")

    def __init__(self, name):
        self.name = name
        self.last_w = None
        self.rd_eng = {}
        self.rd_dma = []


class Op:
    __slots__ = ("eng", "fn", "deps", "sig", "dma", "need", "inc")

    def __init__(self, eng, fn, dma):
        self.eng = eng
        self.fn = fn
        self.deps = {}
        self.sig = None
        self.dma = dma
        self.need = False
        self.inc = 1


class Prog:
    def __init__(self, nc):
        self.nc = nc
        self.ops = []
        self.dma_keys = {}

    def add(self, eng, fn, reads=(), writes=(), dma_key=None):
        op = Op(eng, fn, dma_key is not None)
        for b in reads:
            if b.last_w is not None:
                self._dep(op, b.last_w, True)
        for b in writes:
            if b.last_w is not None:
                self._dep(op, b.last_w, False)
            for r in b.rd_eng.values():
                self._dep(op, r, False)
            for r in b.rd_dma:
                self._dep(op, r, False)
        for b in writes:
            b.last_w = op
            b.rd_eng = {}
            b.rd_dma = []
        for b in reads:
            if b.last_w is op:
                continue
            if op.dma:
                b.rd_dma.append(op)
            else:
                b.rd_eng[eng] = op
        if dma_key is not None:
            ent = self.dma_keys.setdefault(dma_key, [len(self.dma_keys), 0])
            ent[1] += 16
            op.sig = (("dma", ent[0]), ent[1])
            op.inc = 16
        self.ops.append(op)
        return op

    def _dep(self, op, d, raw):
        if d is op:
            return
        if d.eng == op.eng and not d.dma and not op.dma:
            if op.eng == "pe" or (not raw and not STRICT_SYNC):
                return
        op.deps[d] = True
        d.need = True

    def emit(self, final_eng="sp"):
        nc = self.nc
        cnt = {e: 0 for e in ENGS}
        for op in self.ops:
            if not op.dma and op.need:
                cnt[op.eng] += 1
                op.sig = (("eng", op.eng), cnt[op.eng])
        assert len(self.dma_keys) + len(ENGS) <= 100, len(self.dma_keys)
        with contextlib.ExitStack() as stack:
            sems = {}
            for e in ENGS:
                sems[("eng", e)] = stack.enter_context(nc.semaphore("s_" + e))
            for k, (i, _) in self.dma_keys.items():
                sems[("dma", i)] = stack.enter_context(nc.semaphore("d_%d" % i))
            per_eng = {e: [o for o in self.ops if o.eng == e] for e in ENGS}
            finals = [(("dma", i), v) for (i, v) in self.dma_keys.values()]
            know = {e: {} for e in ENGS}
            waits = {}
            wm = {}
            for op in self.ops:
                k = know[op.eng]
                need = {}
                for d in op.deps:
                    s, v = d.sig
                    if need.get(s, (0, None))[0] < v:
                        need[s] = (v, d)
                wl = []
                for s, (v, d) in need.items():
                    if k.get(s, 0) >= v:
                        continue
                    wl.append((s, v))
                    k[s] = v
                    for s2, v2 in wm.get(id(d), {}).items():
                        if k.get(s2, 0) < v2:
                            k[s2] = v2
                waits[id(op)] = wl
                if op.sig is not None and not op.dma:
                    snap = dict(k)
                    snap[op.sig[0]] = op.sig[1]
                    wm[id(op)] = snap
                elif op.sig is not None:
                    snap = dict(k)
                    snap[op.sig[0]] = op.sig[1]
                    wm[id(op)] = snap

            def run(e, engine):
                for op in per_eng[e]:
                    for s, v in waits[id(op)]:
                        engine.wait_ge(sems[s], v)
                    ins = op.fn(engine)
                    if op.sig is not None:
                        ins.then_inc(sems[op.sig[0]], op.inc)
                if e == final_eng:
                    for s, v in finals:
                        engine.wait_ge(sems[s], v)

            with nc.Block() as block:
                block.tensor(lambda eng: run("pe", eng))
                block.scalar(lambda eng: run("act", eng))
                block.vector(lambda eng: run("dve", eng))
                block.gpsimd(lambda eng: run("pool", eng))
                block.sync(lambda eng: run("sp", eng))


def build_program():
    nc = bass.Bass("TRN2", target_bir_lowering=False)
    P = Prog(nc)
    es = contextlib.ExitStack()

    def din(name, shape, dt=F32):
        return nc.dram_tensor(name, shape, dt, kind="ExternalInput").ap()

    def dout(name, shape, dt=F32):
        return nc.dram_tensor(name, shape, dt, kind="ExternalOutput").ap()

    xT_d = din("xT", [D, TT])
    xtok_d = din("xtok", [TT, D])
    ckT_d = din("ckT", [4, 128, PAST])
    cv_d = din("cv", [PAST, 512])
    st_d = din("state", [4, 128, 128])
    win_d = din("w_in", [D, 4096])
    wout_d = din("w_out", [D, D])
    lbl_d = din("lbl", [128, 8])
    ng_d = din("ng", [128, 4])
    lng_d = din("ln_g", [1, D])
    lnb_d = din("ln_b", [1, D])
    cbf_d = din("cbf", [128, 640], BF16)
    cf_d = din("cf", [128, 256])
    y_d = dout("y", [TT, D])
    kp_d = dout("kp", [8, T, 64])
    vp_d = dout("vp", [8, T, 64])
    sp_d = dout("sp", [4, 128, 128])
    ks_d = dout("ks", [8, TS, 64])
    vs_d = dout("vs", [8, TS, 64])
    ss_d = dout("ss", [4, 128, 128])

    AW = 53080
    AR = es.enter_context(nc.sbuf_tensor("arena", [128, AW], F32))
    cur = [0]

    def carve(words):
        o = cur[0]
        cur[0] += words
        assert cur[0] <= AW, cur[0]
        return o

    def f32v(off, n):
        return AR[:, off:off + n]

    def bf16v(off, nwords):
        return AR[:, off:off + nwords].bitcast(BF16)

    o_cbf = carve(320); CBF = bf16v(o_cbf, 320)
    IDENT = CBF[:, 0:128]; TRI = CBF[:, 128:256]; TRIC = CBF[:, 256:384]; ONES = CBF[:, 384:512]
    NEGM = CBF[:, 512:640]
    o_cf = carve(256); CF = f32v(o_cf, 256)
    MASKA = CF[:, 0:128]; MASKH = CF[:, 128:256]
    o_rs = carve(512); RS = f32v(o_rs, 512)
    o_sm = carve(64); SM = f32v(o_sm, 64)
    LBL = SM[:, 0:8]; NG = SM[:, 8:12]; LB = SM[:, 12:16]; A1 = SM[:, 16:20]; A0 = SM[:, 20:24]
    o_mixt = carve(8 * TT // 2); MIXT = bf16v(o_mixt, 8 * TT // 2).rearrange("p (e t) -> p e t", e=8)
    W0 = MIXT[:, 7, 0:2048].rearrange("p (k c) -> p k c", k=8)
    o_wb = carve(4096); WB = bf16v(o_wb, 4096).rearrange("p (s k c) -> p s k c", s=2, k=8)
    o_qs = carve(96); QS = bf16v(o_qs, 64).rearrange("p (h t) -> p h t", h=8); GS = bf16v(o_qs + 64, 32).rearrange("p (a t) -> p a t", a=4)
    o_l4 = carve(80); L4T = [f32v(o_l4 + i * 16, 16) for i in range(5)]
    o_sfp = carve(512); SFP = [f32v(o_sfp + i * 128, 128) for i in range(4)]
    o_rt2 = carve(1024); RT2 = [f32v(o_rt2 + i * 512, 512) for i in range(2)]
    o_i = carve(17 * 256); ITOK = bf16v(o_i, 17 * 256).rearrange("p (t c) -> p t c", t=17)
    o_x = carve(8 * TT // 2); XT = bf16v(o_x, 8 * TT // 2).rearrange("p (k t) -> p k t", k=8)
    XTOK = [f32v(o_x + i * 1024, 1024) for i in range(2)]
    RR = [f32v(o_x + 2048 + i * 1024, 1024) for i in range(2)]
    YY = [f32v(o_x + 4096 + i * 1024, 1024) for i in range(2)]
    o_yy2 = carve(2048)
    YY += [f32v(o_yy2 + i * 1024, 1024) for i in range(2)]
    LNG = f32v(o_x + 6144, 1024)
    LNB = f32v(o_x + 7168, 1024)
    RKW = 18800
    o_k = carve(RKW)
    KT = bf16v(o_k, 4096).rearrange("p (a t) -> p a t", a=4)
    V = bf16v(o_k + 4096, 4096).rearrange("p (t c) -> p t c", t=16)
    KTS = bf16v(o_k + 8192, 2080).rearrange("p (a t) -> p a t", a=4)
    VS = bf16v(o_k + 10272, 2304).rearrange("p (t c) -> p t c", t=9)
    QTZ = [[bf16v(o_k + 12576 + (2 * j + i) * 1032, 1032) for i in range(2)] for j in range(2)]
    GT = [bf16v(o_k + 16704 + j * 1032, 1032) for j in range(2)]
    assert 16704 + 2 * 1032 <= RKW
    hb = o_k
    QH = f32v(hb, TT); OH = QH; hb += TT
    FF = f32v(hb, TT); hb += TT
    SG = f32v(hb, TT); hb += TT
    QE = bf16v(hb, 1032); hb += 1032
    KE = bf16v(hb, 1032); hb += 1032
    KD = bf16v(hb, 1032); hb += 1032
    T2 = [f32v(hb + i * 512, 512) for i in range(2)]; hb += 1024
    GG = f32v(hb, 512); hb += 512
    KK = f32v(hb, 512); hb += 512
    BC = f32v(hb, 512); hb += 512
    EB = f32v(hb, 512); hb += 512
    ENB = f32v(hb, 512); hb += 512
    RT = [GG, KK]
    STM = [bf16v(hb + i * 64, 64) for i in range(2)]; hb += 128
    KETZ = [[bf16v(hb + (2 * i + c) * 64, 64) for c in range(2)] for i in range(2)]; hb += 256
    SBF = bf16v(hb, 33 * 64).rearrange("p (c v) -> p c v", c=33); hb += 33 * 64
    EBL = f32v(hb, 40); hb += 40
    SQ = bf16v(hb, 256); hb += 256
    assert hb <= o_k + RKW, hb - o_k
    o_w = carve(4096)

    def dbl(ap):
        return ap.rearrange("p (h n) -> p h n", h=2)

    E1 = [dbl(bf16v(o_w + i * 512, 512)) for i in range(2)]
    SPB = [dbl(bf16v(o_w + 1024 + i * 512, 512)) for i in range(3)]
    E2 = dbl(bf16v(o_w + 2560, 512))
    AB = [dbl(bf16v(o_w + 3072 + i * 512, 512)) for i in range(2)]
    KF = [f32v(o_w + i * 512, 512) for i in range(2)]
    VF = [f32v(o_w + 1024 + i * 512, 512) for i in range(2)]
    KBF = [bf16v(o_w + 2048 + i * 256, 256) for i in range(2)]

    ZZ = es.enter_context(nc.psum_tensor("zz", [128, 2, 512], F32))
    CC = es.enter_context(nc.psum_tensor("cc", [128, 2, 512], F32))
    OO = es.enter_context(nc.psum_tensor("oo", [128, 2, 512], F32))
    PSG = es.enter_context(nc.psum_tensor("psg", [128, 512], F32))
    PT = es.enter_context(nc.psum_tensor("pt", [128, 1024], BF16))
    PS = [ZZ[:, 0, :], ZZ[:, 1, :], CC[:, 0, :], CC[:, 1, :], OO[:, 0, :], OO[:, 1, :], PSG[:, :]]

    bCONST = Buf("const")
    bWB = [Buf("wb%d" % i) for i in range(2)]
    bXTc = [[Buf("xt%d_%d" % (c, i)) for i in range(8)] for c in range(4)]
    bXT = [b for c in bXTc for b in c]

    def xb(kc, c0_, n_):
        lo = min(c0_ // 512, 3); hi = min((c0_ + n_ - 1) // 512, 3)
        return [bXTc[c][kc] for c in range(lo, hi + 1)]
    bPS = [Buf("ps%d" % i) for i in range(7)]
    _bpt = Buf("pt")
    bPT = [_bpt, _bpt]
    bI = [Buf("i%d" % i) for i in range(17)]
    bKT = [Buf("kt%d" % i) for i in range(16)]
    bV = [Buf("v%d" % i) for i in range(16)]
    bKTS = Buf("kts"); bVS = Buf("vs")
    bKTSn = Buf("ktsn"); bVSn = Buf("vsn")
    bKF = [Buf("kf%d" % i) for i in range(2)]
    bVF = [Buf("vf%d" % i) for i in range(2)]
    bKBF = [Buf("kbf%d" % i) for i in range(2)]
    bQTs = [Buf("qt0"), Buf("qt1")]; bGTs = [Buf("gt0"), Buf("gt1")]
    bE1 = [Buf("e1_%d" % i) for i in range(2)]
    bSPB = [Buf("sp_%d" % i) for i in range(3)]
    bE2 = Buf("e2")
    bAB = [Buf("a_%d" % i) for i in range(2)]
    bMIX = [Buf("mix%d" % i) for i in range(8)]
    bQH = Buf("qh"); bOH = bQH; bFF = Buf("ff"); bSG = Buf("sg")
    bQE = Buf("qe"); bKE = Buf("ke"); bKD = Buf("kd")
    bT2 = [Buf("t2_%d" % i) for i in range(2)]
    bGG = Buf("gg"); bKK = Buf("kk"); bBC = Buf("bc"); bEB = Buf("eb"); bENB = Buf("enb")
    bRT = [bGG, bKK]
    bSTM = [Buf("stm%d" % i) for i in range(2)]
    bKETZ = [Buf("ketz%d" % i) for i in range(2)]
    bSFP = [Buf("sf%d" % i) for i in range(4)]; bSBF = [Buf("sbf%d" % i) for i in range(33)]
    bEBL = Buf("ebl"); bSQ = Buf("sq")
    bSM = Buf("sm"); bRS = Buf("rs")
    bXTOK = [Buf("xtok%d" % i) for i in range(2)]
    bRR = [Buf("rr%d" % i) for i in range(2)]
    bYY = [Buf("yy%d" % i) for i in range(4)]
    bMV = [Buf("mv%d" % i) for i in range(2)]
    bLN = Buf("lngb")
    bQS = Buf("qs")
    bL4 = [Buf("l4_%d" % i) for i in range(5)]
    bRT2 = [Buf("rt2_0"), Buf("rt2_1")]
    bJUNK = Buf("junk")
    regK_attn = bKT + bV + [bKTS, bVS, bKTSn, bVSn] + bQTs + bGTs
    regK_hgrn = ([bQH, bFF, bSG, bQE, bKE, bKD] + bT2 + [bGG, bKK, bBC, bEB, bENB] + bSTM + bKETZ + bSFP
                 + bSBF + [bEBL, bSQ])
    regX_c = bXTOK + bRR + bYY + [bLN]
    regW_p1 = bKF + bVF + bKBF
    regW_at = bE1 + bSPB + [bE2] + bAB

    def op(eng, method, reads, writes, **kw):
        return P.add(eng, lambda e: getattr(e, method)(**kw), reads, writes)

    def dma(eng, out, in_, reads, writes, key):
        return P.add(eng, lambda e: e.dma_start(out=out, in_=in_), reads, writes, dma_key=key)

    def mm(out, lhsT, rhs, start, stop, reads, writes, **kw):
        return P.add("pe", lambda e: e.matmul(out, lhsT=lhsT, rhs=rhs, start=start, stop=stop, **kw),
                     reads, writes)

    def act(out, in_, func, reads, writes, scale=1.0, bias=0.0):
        return P.add("act", lambda e: e.activation(out=out, in_=in_, func=func, bias=bias, scale=scale),
                     reads, writes)

    def alias(new_bufs, old_bufs):
        olds = []
        seen = set()
        for b in old_bufs:
            cands = list(b.rd_eng.values()) + list(b.rd_dma) + ([b.last_w] if b.last_w is not None else [])
            for o in cands:
                if id(o) not in seen:
                    seen.add(id(o))
                    olds.append(o)
        for nb in new_bufs:
            nb.last_w = None
            nb.rd_eng = {}
            nb.rd_dma = list(olds)

    def finish():
        with nc.allow_low_precision("bf16 matmul operands, fp32 accumulation"):
            P.emit()
        es.close()
        return nc

    dma("sp", CBF, cbf_d[:, :], [], [bCONST], "const")
    dma("sp", CF, cf_d[:, :], [], [bCONST], "const")
    dma("sp", LBL, lbl_d[:, :], [], [bCONST], "const")
    dma("sp", NG, ng_d[:, :], [], [bCONST], "const")
    jobs = [(win_d, 0, 512), (win_d, 512, 512)]
    jobs += [(win_d, 1536 + pr * 256, 256) for pr in range(4)]
    jobs += [(win_d, 1024, 512)]
    jobs += [(win_d, 2560 + hd * 384, 384) for hd in range(4)]
    jobs += [(wout_d, 0, 512), (wout_d, 512, 512)]

    loaded = set()

    def load_job(j):
        if j >= len(jobs) or j in loaded:
            return
        loaded.add(j)
        dram, c0, ncols = jobs[j]
        slot = j % 2
        done = 0
        while done < ncols:
            n = min(256, ncols - done)
            dma("pool", WB[:, slot, :, done:done + n],
                dram.rearrange("(k p) c -> p k c", p=128)[:, :, c0 + done:c0 + done + n], [], [bWB[slot]],
                ("wb", slot))
            done += n

    XCB = [(0, 512), (512, 512), (1024, 512), (1536, 528)]
    load_job(0)
    for cb, (c0_, n_) in enumerate(XCB):
        for kc in range(8):
            dma("pool", XT[:, kc, c0_:c0_ + n_], xT_d[kc * 128:(kc + 1) * 128, c0_:c0_ + n_], [], [bXTc[cb][kc]],
                ("xt", cb, kc))
        if cb == 0:
            load_job(1)
            loaded.add(2)
            dma("pool", W0, win_d.rearrange("(k p) c -> p k c", p=128)[:, :, 1536:1792], [], [bMIX[7]], "w0")
    op("pool", "memset", [], [bRS], ap=RS, constant=1.0)
    op("pool", "memset", [bRS], [bRS], ap=RS.rearrange("p (c t) -> p c t", t=64)[:, :, 0:1], constant=0.0)
    for j in range(2):
        for i in range(2):
            op("pool", "memset", [], [bQTs[j]], ap=QTZ[j][i], constant=0.0)
    op("pool", "memset", [], [bQS], ap=QS, constant=0.0)
    op("dve", "tensor_tensor", [bCONST], [bSM], out=A1, in0=LBL[:, 0:4], in1=LBL[:, 4:8], op=ALU.subtract)
    act(A1, A1, AF.Exp, [bSM], [bSM], scale=-1.0)
    op("dve", "tensor_scalar_add", [bSM], [bSM], out=A1, in0=A1, scalar1=1.0)
    op("dve", "reciprocal", [bSM], [bSM], out=LB, in_=A1)
    op("dve", "tensor_scalar", [bSM], [bSM], out=A1, in0=LB, scalar1=-0.5, scalar2=0.5, op0=ALU.mult, op1=ALU.add)
    op("dve", "tensor_scalar", [bSM], [bSM], out=A0, in0=LB, scalar1=0.5, scalar2=0.5, op0=ALU.mult, op1=ALU.add)
    def tok(tt):
        return (tt * 128, 128) if tt < 16 else (T, TS)

    def k_transposes(tt):
        t0, nt = tok(tt)
        s2 = tt % 2
        for j in range(4):
            P.add("pe", lambda e, j=j, s2=s2, nt=nt: e.transpose(
                out=PT[:, s2 * 512 + j * 128:s2 * 512 + j * 128 + nt], in_=KBF[s2][0:nt, j * 128:(j + 1) * 128],
                identity=IDENT[0:nt, 0:nt]), [bKBF[s2], bCONST], [bPT[s2]])
        src = PT[:, s2 * 512:(s2 + 1) * 512].rearrange("p (a t) -> p a t", a=4)[:, :, 0:nt]
        if tt < 16:
            op("dve", "tensor_copy", [bPT[s2]], [bKT[tt]], out=KT[:, :, t0:t0 + nt], in_=src)
        else:
            op("dve", "tensor_copy", [bPT[s2]], [bKTSn], out=KTS[:, :, PAST:PAST + nt], in_=src)

    def proj_tokmajor(g):
        jidx = g if g < 2 else 6
        slot = jidx % 2
        if jidx >= 2:
            load_job(jidx + 1)
        for tt in range(17):
            t0, nt = tok(tt)
            bi = (tt % 4) if g < 2 else 6
            bank = PS[bi]; bb = bPS[bi]
            for kc in range(8):
                mm(bank[0:nt, :], XT[:, kc, t0:t0 + nt], WB[:, slot, kc, :], kc == 0, kc == 7,
                   xb(kc, t0, nt) + [bWB[slot]], [bb])
                if kc == 3 and g == 2:
                    yield
            s2 = tt % 2
            if g == 0:
                op("dve", "tensor_copy", [bb], [bKF[s2]], out=KF[s2][0:nt, :], in_=bank[0:nt, :])
                dst = (kp_d[:, t0:t0 + nt, :] if tt < 16 else ks_d[:, :, :]).rearrange("h t d -> t h d")
                dma("sp", dst, KF[s2][0:nt, :].rearrange("t (h d) -> t h d", h=8), [bKF[s2]], [], ("kf", s2))
                if tt < 16:
                    act(KBF[s2][0:nt, :], KF[s2][0:nt, :], AF.Identity, [bKF[s2]], [bKBF[s2]])
                else:
                    op("pool", "tensor_copy", [bKF[s2]], [bKBF[s2]], out=KBF[s2][0:nt, :], in_=KF[s2][0:nt, :])
                if tt >= 1:
                    k_transposes(tt - 1)
                if tt == 16:
                    k_transposes(16)
            elif g == 1:
                op("dve", "tensor_copy", [bb], [bVF[s2]], out=VF[s2][0:nt, :], in_=bank[0:nt, :])
                dst = (vp_d[:, t0:t0 + nt, :] if tt < 16 else vs_d[:, :, :]).rearrange("h t d -> t h d")
                dma("sp", dst, VF[s2][0:nt, :].rearrange("t (h d) -> t h d", h=8), [bVF[s2]], [], ("vf", s2))
                if tt < 16:
                    act(V[0:nt, tt, :], VF[s2][0:nt, :], AF.Identity, [bVF[s2]], [bV[tt]])
                else:
                    op("pool", "tensor_copy", [bVF[s2]], [bVSn], out=VS[0:nt, 8, :], in_=VF[s2][0:nt, :])
            else:
                op("dve", "tensor_copy", [bb], [bI[tt]], out=ITOK[0:nt, tt, :], in_=bank[0:nt, :])
            yield

    col_tiles = [(i * 512, 512) for i in range(4)] + [(T, TS)]
    H0 = slice(0, 64); H1 = slice(64, 128)

    def inproj_pair(pr):
        jidx = 2 + pr
        slot = jidx % 2
        jb = pr % 2
        if pr > 0:
            load_job(jidx + 1)
        wq_ = W0[:, :, 0:128] if pr == 0 else WB[:, slot, :, 0:128]
        wg_ = W0[:, :, 128:256] if pr == 0 else WB[:, slot, :, 128:256]
        bw_ = bMIX[7] if pr == 0 else bWB[slot]
        for ci, (c0, n) in enumerate(col_tiles):
            bank = PS[6]; bb = bPS[6]
            for kc in range(8):
                mm(bank[:, 0:n], wq_[:, kc, :], XT[:, kc, c0:c0 + n], kc == 0, kc == 7,
                   xb(kc, c0, n) + [bw_], [bb])
                if kc == 3:
                    yield
            op("dve", "tensor_scalar_mul", [bb], [bQTs[jb]], out=QTZ[jb][0][H0, c0:c0 + n], in0=bank[H0, 0:n],
               scalar1=0.125)
            op("dve", "tensor_scalar_mul", [bb], [bQTs[jb]], out=QTZ[jb][1][H1, c0:c0 + n], in0=bank[H1, 0:n],
               scalar1=0.125)
            yield
            for kc in range(8):
                mm(bank[:, 0:n], wg_[:, kc, :], XT[:, kc, c0:c0 + n], kc == 0, kc == 7,
                   xb(kc, c0, n) + [bw_], [bb])
                if kc == 3:
                    yield
            if pr == 0:
                act(GT[jb][:, c0:c0 + n], bank[:, 0:n], AF.Silu, [bb], [bGTs[jb]])
            else:
                op("dve", "tensor_copy", [bb], [bGTs[jb]], out=GT[jb][:, c0:c0 + n], in_=bank[:, 0:n])
            yield
        if pr != 0:
            act(GT[jb][:, 0:TT], GT[jb][:, 0:TT], AF.Silu, [bGTs[jb]], [bGTs[jb]])
        for hh in range(2):
            op("pool", "tensor_copy", [bQTs[jb]], [bQS], out=QS[:, 2 * pr + hh, :], in_=QTZ[jb][hh][:, T:TT])
        op("pool", "tensor_copy", [bGTs[jb]], [bQS], out=GS[:, pr, :], in_=GT[jb][:, T:TT])
        yield

    gk = proj_tokmajor(0)
    gv = proj_tokmajor(1)
    gi = inproj_pair(0)
    for blk in range(5):
        ntl = 4 if blk < 4 else 1
        for _ in range(ntl):
            next(gk)
        for _ in range(ntl):
            next(gv)
        for _ in range(4):
            next(gi)
    for g_ in (gk, gv):
        for _ in g_:
            pass
    load_job(3)
    for _ in gi:
        pass

    if os.environ.get('KSTOP') == '1':
        return finish()
    alias(regW_at, regW_p1)
    for pr in range(4):
        dma("pool", KTS[:, pr, 0:PAST], ckT_d[pr, :, :], [], [bKTS], "kts")
    for hf in range(2):
        dma("pool", VS[:, hf * 4:(hf + 1) * 4, :],
            cv_d.rearrange("(k p) c -> p k c", p=128)[:, hf * 4:(hf + 1) * 4, :], [], [bVS], "vs")

    side = []

    def side_step():
        while side:
            try:
                next(side[0])
                return
            except StopIteration:
                side.pop(0)

    tiles = []
    for pr in range(4):
        pc = slice(pr * 128, (pr + 1) * 128)
        for Q in range(4):
            nb = 4 * Q + 4
            for bi, kb in enumerate(range(4 * Q + 3, -1, -1)):
                tiles.append(dict(pr=pr, jb=pr % 2, pstart=(Q == 0 and bi == 0), q0=512 * Q, N=512,
                                  kt=KT[:, pr, kb * 128:(kb + 1) * 128], v=V[:, kb, pc], nk=128,
                                  c0=max(0, 128 * (kb - 4 * Q)), diag=kb >= 4 * Q, mw=128, rd=[bKT[kb], bV[kb]],
                                  first=bi == 0, last=bi == nb - 1))
    nt_ = len(tiles)
    for i, t in enumerate(tiles):
        t["e1"] = i % 2; t["sp"] = i % 3; t["a"] = i % 2
    bZ = [bPS[0], bPS[1]]; bC = [bPS[2], bPS[3]]; bO = [bPS[4], bPS[5]]

    def st_z(t):
        nk = t["nk"]; c0 = t["c0"]; N = t["N"]; q0 = t["q0"]; jb = t["jb"]
        for hh in range(2):
            mm(ZZ[0:nk, hh, c0:N], t["kt"], QTZ[jb][hh][:, q0 + c0:q0 + N], True, not t["diag"],
               t["rd"] + [bQTs[jb]], [bZ[hh]], skip_group_check=True)
            if t["diag"]:
                mw = t["mw"]
                mm(ZZ[0:nk, hh, c0:c0 + mw], IDENT[0:nk, 0:nk], NEGM[0:nk, 0:mw], False, True, [bCONST], [bZ[hh]],
                   skip_group_check=True)
        e1 = E1[t["e1"]]; be1 = bE1[t["e1"]]
        act(e1[0:nk, :, c0:N], ZZ[0:nk, :, c0:N], AF.Exp, bZ, [be1])

    def st_ln(t):
        nk = t["nk"]; c0 = t["c0"]; N = t["N"]
        e1 = E1[t["e1"]]; be1 = bE1[t["e1"]]
        sp = SPB[t["sp"]]
        act(sp[0:nk, :, c0:N], e1[0:nk, :, c0:N], AF.Ln, [be1], [bSPB[t["sp"]]], bias=1.0)

    def st_tric(t):
        if t["last"]:
            return
        nk = t["nk"]; c0 = t["c0"]; N = t["N"]
        sp = SPB[t["sp"]]
        for hh in range(2):
            mm(CC[:, hh, c0:N], TRIC[0:nk, :], sp[0:nk, hh, c0:N], False, False, [bSPB[t["sp"]], bCONST],
               [bC[hh]], skip_group_check=True)

    def st_tri(t):
        nk = t["nk"]; c0 = t["c0"]; N = t["N"]
        sp = SPB[t["sp"]]
        for hh in range(2):
            mm(CC[:, hh, c0:N], TRI[0:nk, :], sp[0:nk, hh, c0:N], t["first"], False, [bSPB[t["sp"]], bCONST],
               [bC[hh]], skip_group_check=True)
        act(E2[0:nk, :, c0:N], CC[0:nk, :, c0:N], AF.Exp, bC, [bE2], scale=-1.0)

    def st_mul(t):
        nk = t["nk"]; c0 = t["c0"]; N = t["N"]
        a = AB[t["a"]]
        op("dve", "tensor_tensor", [bE1[t["e1"]], bE2], [bAB[t["a"]]], out=a[0:nk, :, c0:N],
           in0=E1[t["e1"]][0:nk, :, c0:N], in1=E2[0:nk, :, c0:N], op=ALU.mult)

    def st_av(t):
        nk = t["nk"]; c0 = t["c0"]; N = t["N"]; q0 = t["q0"]; jb = t["jb"]; pr = t["pr"]
        a = AB[t["a"]]
        for hh in range(2):
            mm(OO[:, hh, c0:N], t["v"], a[0:nk, hh, c0:N], t["first"], t["last"], t["rd"] + [bAB[t["a"]]],
               [bO[hh]], skip_group_check=True)
        if t["last"]:
            for hh, rows in ((0, H0), (1, H1)):
                op("dve", "tensor_tensor", [bO[hh], bGTs[jb]], [bMIX[pr]], out=MIXT[rows, pr, q0:q0 + N],
                   in0=OO[rows, hh, 0:N], in1=GT[jb][rows, q0:q0 + N], op=ALU.mult)

    for step in range(nt_ + 2):
        if step < nt_:
            t = tiles[step]
            if t["pstart"]:
                while side:
                    side_step()
                side.append(inproj_pair(t["pr"] + 1) if t["pr"] < 3 else proj_tokmajor(2))
            st_z(t)
        if 0 <= step - 2 < nt_:
            st_tric(tiles[step - 2])
        if 0 <= step - 1 < nt_:
            st_tri(tiles[step - 1])
        if step < nt_:
            st_ln(tiles[step])
        if 0 <= step - 1 < nt_:
            st_mul(tiles[step - 1])
        if 0 <= step - 2 < nt_:
            st_av(tiles[step - 2])
        side_step()
    while side:
        side_step()

    stiles = [dict(kts=lambda a: KTS[:, a, PAST:PAST + TS], vs=lambda a: VS[0:TS, 8, a * 128:(a + 1) * 128], nk=TS,
                   diag=True, rd=[bKTSn, bVSn], first=True, last=False)]
    for bi, kb in enumerate(range(7, -1, -1)):
        stiles.append(dict(kts=lambda a, kb=kb: KTS[:, a, kb * 128:(kb + 1) * 128],
                           vs=lambda a, kb=kb: VS[:, kb, a * 128:(a + 1) * 128], nk=128, diag=False,
                           rd=[bKTS, bVS], first=False, last=bi == 7))
    for i, t in enumerate(stiles):
        t["e1"] = i % 2; t["sp"] = i % 3; t["a"] = i % 2
    ZSb = ZZ[:, 0, 0:128]; CSb = CC[:, 0, 0:128]; OSb = OO[:, 0, 0:128]
    W8 = 8 * TS

    def ss_z(t):
        nk = t["nk"]
        for h in range(8):
            hs = slice(h * TS, (h + 1) * TS)
            mm(ZSb[0:nk, hs], t["kts"](h // 2), QS[:, h, :], h == 0, (not t["diag"]) and h == 7, t["rd"] + [bQS],
               [bPS[0]], skip_group_check=True)
            if t["diag"]:
                mm(ZSb[0:nk, hs], IDENT[0:nk, 0:nk], NEGM[0:nk, 0:TS], False, h == 7, [bCONST], [bPS[0]],
                   skip_group_check=True)
        act(E1[t["e1"]][0:nk, 0, 0:W8], ZSb[0:nk, :], AF.Exp, [bPS[0]], [bE1[t["e1"]]])

    def ss_ln(t):
        nk = t["nk"]
        act(SPB[t["sp"]][0:nk, 0, 0:W8], E1[t["e1"]][0:nk, 0, 0:W8], AF.Ln, [bE1[t["e1"]]], [bSPB[t["sp"]]], bias=1.0)

    def ss_tric(t):
        if t["last"]:
            return
        nk = t["nk"]
        mm(CSb, TRIC[0:nk, :], SPB[t["sp"]][0:nk, 0, 0:W8], False, False, [bSPB[t["sp"]], bCONST], [bPS[2]],
           skip_group_check=True)

    def ss_tri(t):
        nk = t["nk"]
        mm(CSb, TRI[0:nk, :], SPB[t["sp"]][0:nk, 0, 0:W8], t["first"], False, [bSPB[t["sp"]], bCONST], [bPS[2]],
           skip_group_check=True)
        act(E2[0:nk, 0, 0:W8], CSb[0:nk, :], AF.Exp, [bPS[2]], [bE2], scale=-1.0)

    def ss_mul(t):
        nk = t["nk"]
        op("dve", "tensor_tensor", [bE1[t["e1"]], bE2], [bAB[t["a"]]], out=AB[t["a"]][0:nk, 0, 0:W8],
           in0=E1[t["e1"]][0:nk, 0, 0:W8], in1=E2[0:nk, 0, 0:W8], op=ALU.mult)

    def ss_av(t):
        nk = t["nk"]
        for h in range(8):
            hs = slice(h * TS, (h + 1) * TS)
            mm(OSb[:, hs], t["vs"](h // 2), AB[t["a"]][0:nk, 0, hs], t["first"] and h == 0, t["last"] and h == 7,
               t["rd"] + [bAB[t["a"]]], [bPS[4]], skip_group_check=True)
        if t["last"]:
            for h in range(8):
                rows = H0 if h % 2 == 0 else H1
                op("dve", "tensor_tensor", [bPS[4], bQS], [bMIX[h // 2]], out=MIXT[rows, h // 2, T:TT],
                   in0=OSb[rows, h * TS:(h + 1) * TS], in1=GS[rows, h // 2, :], op=ALU.mult)

    ns_ = len(stiles)
    for step in range(ns_ + 2):
        if step < ns_:
            ss_z(stiles[step])
        if 0 <= step - 2 < ns_:
            ss_tric(stiles[step - 2])
        if 0 <= step - 1 < ns_:
            ss_tri(stiles[step - 1])
        if step < ns_:
            ss_ln(stiles[step])
        if 0 <= step - 1 < ns_:
            ss_mul(stiles[step - 1])
        if 0 <= step - 2 < ns_:
            ss_av(stiles[step - 2])

    if os.environ.get('KSTOP') == 'A':
        return finish()
    alias(regK_hgrn, regK_attn)
    bHprev = []
    bHprev2 = []
    for i in range(2):
        for c in range(2):
            op("pool", "memset", [], [bKETZ[i]], ap=KETZ[i][c], constant=0.0)
    NCT = len(col_tiles)

    def ctile(tt):
        return min(tt // 4, 4)

    for hd in range(4):
        slot = (7 + hd) % 2
        load_job(7 + hd + 1)
        wq = WB[:, slot, :, 0:128]; wf = WB[:, slot, :, 128:256]; wg = WB[:, slot, :, 256:384]
        bQHc = [Buf("qh%d" % i) for i in range(NCT)]
        bFFc = [Buf("ff%d" % i) for i in range(NCT)]
        bSGc = [Buf("sg%d" % i) for i in range(NCT)]
        bQEc = [Buf("qe%d" % i) for i in range(NCT)]
        bKEc = [Buf("ke%d" % i) for i in range(NCT)]
        bKDc = [Buf("kd%d" % i) for i in range(NCT)]
        bEBLc = [Buf("ebl%d" % i) for i in range(NCT)]
        alias(bQHc + bFFc + bSGc, [bQH, bFF, bSG] + bHprev)
        alias(bQEc + bKEc + bKDc + bEBLc, [bQE, bKE, bKD, bEBL] + bHprev2)
        bHprev = bQHc + bFFc + bSGc
        bHprev2 = bQEc + bKEc + bKDc + bEBLc
        bi_ = 0
        for ci, (c0, n) in enumerate(col_tiles):
            cs = slice(c0, c0 + n)
            for (w, kind) in ((wq, "q"), (wf, "f"), (wg, "g")):
                bank = PS[bi_ % 4]; bb = bPS[bi_ % 4]; bi_ += 1
                for kc in range(8):
                    mm(bank[:, 0:n], w[:, kc, :], XT[:, kc, cs], kc == 0, kc == 7, xb(kc, c0, n) + [bWB[slot]], [bb])
                if kind == "q":
                    act(QH[:, cs], bank[:, 0:n], AF.Silu, [bb], [bQHc[ci]])
                elif kind == "g":
                    act(SG[:, cs], bank[:, 0:n], AF.Silu, [bb], [bSGc[ci]])
                else:
                    t2 = T2[ci % 2]; bt2 = bT2[ci % 2]
                    act(t2[:, 0:n], bank[:, 0:n], AF.Tanh, [bb], [bt2], scale=0.5)
                    op("dve", "tensor_scalar", [bt2, bSM], [bFFc[ci]], out=FF[:, cs], in0=t2[:, 0:n],
                       scalar1=A1[:, hd:hd + 1], scalar2=A0[:, hd:hd + 1], op0=ALU.mult, op1=ALU.add)

        hv = slice(hd * 128, (hd + 1) * 128)
        if hd == 3:
            alias(regX_c, bXT)
            load_job(12)
            dma("sp", LNG, lng_d[0:1, :].partition_broadcast(128), [], [bLN], "lngb")
            dma("sp", LNB, lnb_d[0:1, :].partition_broadcast(128), [], [bLN], "lngb")
            for tt_ in range(2):
                t0_, nt_c = tok(tt_)
                dma("sp", XTOK[tt_][0:nt_c, :], xtok_d[t0_:t0_ + nt_c, :], [], [bXTOK[tt_]], ("xtok", tt_))

        GG_, KK_, BC_, EB_, ENB_ = GG, KK, BC, EB, ENB
        bGG_, bKK_, bBC_, bEB_, bENB_ = bGG, bKK, bBC, bEB, bENB

        def Lst(ci):
            c0, n = col_tiles[ci]
            cs = slice(c0, c0 + n)
            if ci == 4:
                GG, KK, BC, EB, ENB = L4T
                bGG, bKK, bBC, bEB, bENB = bL4
            else:
                GG, KK, BC, EB, ENB = GG_, KK_, BC_, EB_, ENB_
                bGG, bKK, bBC, bEB, bENB = bGG_, bKK_, bBC_, bEB_, bENB_

            def l1():
                act(GG[:, 0:n], FF[:, cs], AF.Ln, [bFFc[ci]], [bGG])
                act(KK[:, 0:n], FF[:, cs], AF.Identity, [bFFc[ci]], [bKK], scale=-1.0, bias=1.0)

            def l2():
                op("dve", "tensor_tensor_scan", [bRS, bGG], [bBC], out=BC[:, 0:n], data0=RS[:, 0:n], data1=GG[:, 0:n],
                   initial=0.0, op0=ALU.mult, op1=ALU.add)

            def l3():
                act(EB[:, 0:n], BC[:, 0:n], AF.Exp, [bBC], [bEB])
                act(ENB[:, 0:n], BC[:, 0:n], AF.Exp, [bBC], [bENB], scale=-1.0)

            def l4():
                op("pool", "tensor_tensor", [bQHc[ci], bEB], [bQEc[ci]], out=QE[:, cs], in0=QH[:, cs], in1=EB[:, 0:n],
                   op=ALU.mult)
                op("dve", "tensor_tensor", [bKK, bENB], [bKEc[ci]], out=KE[:, cs], in0=KK[:, 0:n], in1=ENB[:, 0:n],
                   op=ALU.mult)
                if n == 512:
                    op("pool", "tensor_copy", [bEB], [bEBLc[ci]], out=EBL[:, ci * 8:ci * 8 + 8],
                       in_=EB.rearrange("p (c t) -> p c t", t=64)[:, :, 63])
                else:
                    op("pool", "tensor_copy", [bEB], [bEBLc[ci]], out=EBL[:, 32:33], in_=EB[:, n - 1:n])

            def l5():
                if n == 512:
                    op("dve", "tensor_tensor", [bKEc[ci], bEBLc[ci]], [bKDc[ci]],
                       out=KD[:, cs].rearrange("p (c t) -> p c t", t=64),
                       in0=KE[:, cs].rearrange("p (c t) -> p c t", t=64),
                       in1=EBL[:, ci * 8:ci * 8 + 8].unsqueeze(2).to_broadcast([128, 8, 64]), op=ALU.mult)
                else:
                    op("pool", "tensor_scalar", [bKEc[ci], bEBLc[ci]], [bKDc[ci]], out=KD[:, cs], in0=KE[:, cs],
                       scalar1=EBL[:, 32:33], scalar2=None, op0=ALU.mult)

            return [l1, l2, l3, l4, l5]

        def Mst(ci):
            c0, n = col_tiles[ci]
            cs = slice(c0, c0 + n)
            bank = PS[ci % 2]; bb = bPS[ci % 2]
            r0 = RT2[0]; r1 = RT2[1]

            def m1():
                act(SQ[:, 0:n], OH[:, cs], AF.Square, [bQHc[ci]], [bSQ])

            def m2():
                mm(bank[:, 0:n], ONES, SQ[:, 0:n], True, True, [bSQ, bCONST], [bb])
                act(r0[:, 0:n], bank[:, 0:n], AF.Ln, [bb], [bRT2[0]], scale=1.0 / 128.0, bias=RMS_EPS)

            def m3():
                act(r0[:, 0:n], r0[:, 0:n], AF.Exp, [bRT2[0]], [bRT2[0]], scale=-0.5)

            def m4():
                op("dve", "tensor_tensor", [bQHc[ci], bRT2[0]], [bRT2[1]], out=r1[:, 0:n], in0=OH[:, cs],
                   in1=r0[:, 0:n], op=ALU.mult)
                op("dve", "scalar_tensor_tensor", [bRT2[1], bCONST, bSGc[ci]], [bMIX[4 + hd]],
                   out=MIXT[:, 4 + hd, cs], in0=r1[:, 0:n], scalar=NG[:, hd:hd + 1], in1=SG[:, cs], op0=ALU.mult,
                   op1=ALU.mult)

            return [m1, m2, m3, m4]

        def mk_recur(first_zero):
            st = dict(ch2=0, ch3=0, par=spar[0])

            def p2a(tt):
                t0, nt = tok(tt)
                k2 = tt % 2
                ci = ctile(tt)
                P.add("pe", lambda e, k2=k2, t0=t0, nt=nt: e.transpose(
                    out=PT[0:nt, k2 * 512:k2 * 512 + 128], in_=KD[:, t0:t0 + nt], identity=IDENT[:, :]),
                    [bKDc[ci], bCONST], [bPT[k2]])
                nch = max(1, nt // 64)
                for c in range(nch):
                    cl = min(64, nt)
                    rs_ = slice(c * 64, c * 64 + cl)
                    if nt == 128:
                        act(KETZ[k2][c][rs_, :], PT[rs_, k2 * 512:k2 * 512 + 128], AF.Identity, [bPT[k2]], [bKETZ[k2]])
                    else:
                        op("dve", "tensor_copy", [bPT[k2]], [bKETZ[k2]], out=KETZ[k2][c][rs_, :],
                           in_=PT[rs_, k2 * 512:k2 * 512 + 128])

            def p2b(tt):
                t0, nt = tok(tt)
                k2 = tt % 2
                ci = ctile(tt)
                nch = max(1, nt // 64)
                for c in range(nch):
                    ch = st["ch2"]
                    U = PS[3 + ch % 2]; bU = bPS[3 + ch % 2]
                    if nt == 128:
                        mm(U[:, 0:128], KETZ[k2][c][:, :], ITOK[:, tt, hv], True, True, [bKETZ[k2], bI[tt]], [bU])
                    else:
                        mm(U[:, 0:128], KETZ[k2][c][0:nt, :], ITOK[0:nt, tt, hv], True, True, [bKETZ[k2], bI[tt]], [bU])
                    ebc = EBL[:, (tt * 2 + c):(tt * 2 + c) + 1] if tt < 16 else EBL[:, 32:33]
                    so = st["par"]; sn = (so + 1) % 4
                    if first_zero and ch == 0:
                        op("dve", "tensor_copy", [bU], [bSFP[sn]], out=SFP[sn], in_=U[:, 0:128])
                    else:
                        op("dve", "scalar_tensor_tensor", [bU, bEBLc[ci], bSFP[so]], [bSFP[sn]], out=SFP[sn],
                           in0=SFP[so], scalar=ebc, in1=U[:, 0:128], op0=ALU.mult, op1=ALU.add)
                    st["par"] = sn
                    st["ch2"] = ch + 1
                    op("pool", "tensor_copy", [bSFP[sn]], [bSBF[ch + 1]], out=SBF[:, ch + 1, :], in_=SFP[sn])

            def p3a(tt):
                t0, nt = tok(tt)
                k2 = tt % 2
                ci = ctile(tt)
                Sb = PS[2]; bSb = bPS[2]
                mm(Sb[0:nt, 0:nt], KE[:, t0:t0 + nt], QE[:, t0:t0 + nt], True, True, [bKEc[ci], bQEc[ci]], [bSb])
                op("dve", "tensor_tensor", [bSb, bCONST], [bSTM[k2]], out=STM[k2][0:nt, 0:nt], in0=Sb[0:nt, 0:nt],
                   in1=MASKH[0:nt, 0:nt], op=ALU.mult)

            def p3(tt):
                t0, nt = tok(tt)
                k2 = tt % 2
                ci = ctile(tt)
                ch = st["ch3"]
                Ob = PS[5 + k2]; bOb = bPS[5 + k2]
                nch = max(1, nt // 64)
                terms = [c for c in range(nch) if not (first_zero and ch + c == 0)]
                mm(Ob[:, 0:nt], ITOK[0:nt, tt, hv], STM[k2][0:nt, 0:nt], True, len(terms) == 0,
                   [bI[tt], bSTM[k2]], [bOb], skip_group_check=True)
                for c in terms:
                    cl = min(64, nt)
                    mm(Ob[:, c * 64:c * 64 + cl], SBF[:, ch + c, :], QE[:, t0 + c * 64:t0 + c * 64 + cl],
                       False, c == terms[-1], [bSBF[ch + c], bQEc[ci]], [bOb], skip_group_check=True)
                st["ch3"] = ch + nch
                act(OH[:, t0:t0 + nt], Ob[:, 0:nt], AF.Identity, [bOb], [bQHc[ci]])

            return p2a, p2b, p3a, p3, st

        spar = [0]
        sched = {}

        def at(it, fn):
            sched.setdefault(it, []).append(fn)

        dma("sp", SFP[0], st_d[hd, :, :], [], [bSFP[0]], ("sfi", 0))
        op("pool", "tensor_copy", [bSFP[0]], [bSBF[0]], out=SBF[:, 0, :], in_=SFP[0])
        for f0, f4 in zip(Lst(0), Lst(4)):
            f0()
            f4()
        p2a, p2b, p3a, p3, st = mk_recur(False)
        l1s = Lst(1)
        l1s[0]()
        p2a(16)
        l1s[1]()
        p3a(16)
        l1s[2]()
        p2b(16)
        l1s[3]()
        p3(16)
        l1s[4]()
        spar[0] = st["par"]
        dma("sp", ss_d[hd, :, :], SFP[spar[0]], [bSFP[spar[0]]], [], ("sfo", spar[0]))
        for ci in (2, 3):
            for k_, fn in enumerate(Lst(ci)):
                at(4 * (ci - 2) + k_, fn)
        for k_, fn in enumerate(Mst(4)):
            at(1 + k_, fn)
        for ci in (0, 1, 2, 3):
            for k_, fn in enumerate(Mst(ci)):
                at(4 * (ci + 1) + 1 + k_, fn)
        p2a, p2b, p3a, p3, st = mk_recur(True)
        p2a(0)
        for tt in range(16):
            p2b(tt)
            if tt + 1 < 16:
                p2a(tt + 1)
            if tt >= 1:
                p3(tt - 1)
            p3a(tt)
            for fn in sched.pop(tt, []):
                fn()
        p3(15)
        spar[0] = st["par"]
        dma("sp", sp_d[hd, :, :], SFP[spar[0]], [bSFP[spar[0]]], [], ("sfo", spar[0]))
        for it in sorted(sched):
            for fn in sched[it]:
                fn()

    if os.environ.get('KSTOP') == 'B':
        return finish()
    wslot = [11 % 2, 12 % 2]
    MV = SM[:, 24:56]
    def c_stage_a(tt):
        t0, nt = tok(tt)
        s2 = tt % 2
        for h2 in range(2):
            bi = (tt * 2 + h2) % 4
            bank = PS[bi]; bb = bPS[bi]
            for ec in range(8):
                mm(bank[0:nt, :], MIXT[:, ec, t0:t0 + nt], WB[:, wslot[h2], ec, :], ec == 0, ec == 7,
                   [bMIX[ec], bWB[wslot[h2]]], [bb])
            op("dve", "scalar_tensor_tensor", [bXTOK[s2], bb], [bRR[s2]], out=RR[s2][0:nt, h2 * 512:(h2 + 1) * 512],
               in0=XTOK[s2][0:nt, h2 * 512:(h2 + 1) * 512], scalar=ALPHA, in1=bank[0:nt, :], op0=ALU.mult,
               op1=ALU.add)
        if tt + 2 < 17:
            t2_, n2_ = tok(tt + 2)
            dma("sp", XTOK[s2][0:n2_, :], xtok_d[t2_:t2_ + n2_, :], [], [bXTOK[s2]], ("xtok", s2))
        st6 = MV[:, s2 * 16:s2 * 16 + 12].rearrange("p (a b) -> p a b", a=2)
        mv = MV[:, s2 * 16 + 12:s2 * 16 + 14]
        rstd = MV[:, s2 * 16 + 14:s2 * 16 + 15]
        for h2 in range(2):
            op("dve", "bn_stats", [bRR[s2]], [bMV[s2]], out=st6[0:nt, h2, :], in_=RR[s2][0:nt, h2 * 512:(h2 + 1) * 512])
        op("dve", "bn_aggr", [bMV[s2]], [bMV[s2]], out=mv[0:nt, :], in_=st6[0:nt, :, :])
        act(rstd[0:nt, :], mv[0:nt, 1:2], AF.Ln, [bMV[s2]], [bMV[s2]], bias=LN_EPS)
        act(rstd[0:nt, :], rstd[0:nt, :], AF.Exp, [bMV[s2]], [bMV[s2]], scale=-0.5)
        if tt >= 1:
            c_gamma(tt - 1)
        nmr = MV[:, s2 * 16 + 15:s2 * 16 + 16]
        if nt == 128:
            op("dve", "scalar_tensor_tensor", [bMV[s2]], [bMV[s2]], out=nmr[0:nt, :], in0=mv[0:nt, 0:1], scalar=-1.0,
               in1=rstd[0:nt, :], op0=ALU.mult, op1=ALU.mult)
            P.add("act", lambda e, s2=s2, nt=nt, nmr=nmr, rstd=rstd, s4=tt % 4: e.activation(
                out=YY[s4][0:nt, :], in_=RR[s2][0:nt, :], func=AF.Identity, bias=nmr[0:nt, :], scale=rstd[0:nt, :]),
                [bRR[s2], bMV[s2]], [bYY[tt % 4]])
        else:
            op("dve", "tensor_scalar", [bRR[s2], bMV[s2]], [bYY[tt % 4]], out=YY[tt % 4][0:nt, :], in0=RR[s2][0:nt, :],
               scalar1=mv[0:nt, 0:1], scalar2=rstd[0:nt, :], op0=ALU.subtract, op1=ALU.mult)

    def c_gamma(tt):
        t0, nt = tok(tt)
        s2 = tt % 2
        op("dve", "tensor_tensor", [bYY[tt % 4], bLN], [bYY[tt % 4]], out=YY[tt % 4][0:nt, :], in0=YY[tt % 4][0:nt, :],
           in1=LNG[0:nt, :], op=ALU.mult)

    def c_stage_b(tt):
        t0, nt = tok(tt)
        s2 = tt % 2
        op("pool", "tensor_tensor", [bYY[tt % 4], bLN], [bYY[tt % 4]], out=YY[tt % 4][0:nt, :], in0=YY[tt % 4][0:nt, :],
           in1=LNB[0:nt, :], op=ALU.add)
        dma("pool", y_d[t0:t0 + nt, :], YY[tt % 4][0:nt, :], [bYY[tt % 4]], [], ("yy", tt % 4))

    for tt in range(18):
        if tt < 17:
            c_stage_a(tt)
        else:
            c_gamma(16)
        if tt >= 1:
            c_stage_b(tt - 1)

    return finish()


_NC_CACHE = {}


def _consts():
    bf = ml_dtypes.bfloat16
    j = np.arange(128)
    ident = np.eye(128, dtype=np.float32)
    tri = (j[:, None] >= j[None, :]).astype(np.float32)
    maska = (j[:, None] < j[None, :]).astype(np.float32)
    negm = (maska - 1.0) * 30000.0
    cbf = np.concatenate([ident, tri, 1.0 - tri, np.ones((128, 128), np.float32), negm], axis=1).astype(bf)
    maskh = ((j[:, None] // 64 == j[None, :] // 64) & (j[:, None] <= j[None, :])).astype(np.float32)
    cf = np.concatenate([maska, maskh], axis=1).astype(np.float32)
    return cbf, cf


def kernel(x_prompt, x_sample, cache_k, cache_v, state_s, w_in, w_out, lb_logits, hgrn_norm_g, ln_g, ln_b):
    f32 = np.float32
    x_prompt = np.asarray(x_prompt, f32); x_sample = np.asarray(x_sample, f32)
    cache_k = np.asarray(cache_k, f32); cache_v = np.asarray(cache_v, f32)
    state_s = np.asarray(state_s, f32)
    w_in0 = np.asarray(w_in, f32)[0]; w_out0 = np.asarray(w_out, f32)[0]
    cols = [np.arange(512, 1024), np.arange(1024, 1536), np.arange(3072, 3584)]
    for pr in range(4):
        cols += [np.arange(pr * 128, (pr + 1) * 128), np.arange(1536 + pr * 128, 1536 + (pr + 1) * 128)]
    for hd in range(4):
        cols += [np.arange(2048 + hd * 128, 2048 + (hd + 1) * 128), np.arange(2560 + hd * 128, 2560 + (hd + 1) * 128),
                 np.arange(3584 + hd * 128, 3584 + (hd + 1) * 128)]
    w_in_r = np.ascontiguousarray(w_in0[:, np.concatenate(cols)])
    lbl = np.ascontiguousarray(np.asarray(lb_logits, f32).reshape(2, 4, 128).transpose(2, 0, 1).reshape(128, 8))
    ng = np.ascontiguousarray(np.asarray(hgrn_norm_g, f32).reshape(4, 128).T)
    cbf, cf = _consts()
    in_maps = []
    for b in range(NCORES):
        xtok = np.concatenate([x_prompt[b], x_sample[b]], axis=0)
        in_maps.append({
            "xT": np.ascontiguousarray(xtok.T),
            "xtok": np.ascontiguousarray(xtok),
            "ckT": np.ascontiguousarray(cache_k[0, b].transpose(0, 2, 1).reshape(4, 128, PAST)),
            "cv": np.ascontiguousarray(cache_v[0, b].transpose(1, 0, 2).reshape(PAST, 512)),
            "state": np.ascontiguousarray(state_s[0, b]),
            "w_in": w_in_r, "w_out": w_out0, "lbl": lbl, "ng": ng,
            "ln_g": np.asarray(ln_g, f32).reshape(1, D), "ln_b": np.asarray(ln_b, f32).reshape(1, D),
            "cbf": cbf, "cf": cf,
        })
    if "nc" not in _NC_CACHE:
        _NC_CACHE["nc"] = build_program()
    res = run_bass_kernel_spmd(_NC_CACHE["nc"], in_maps, core_ids=list(range(NCORES)))
    r = res.results
    y = np.stack([r[b]["y"] for b in range(NCORES)])
    return (np.ascontiguousarray(y[:, :T]), np.ascontiguousarray(y[:, T:]),
            np.stack([r[b]["kp"] for b in range(NCORES)])[None],
            np.stack([r[b]["vp"] for b in range(NCORES)])[None],
            np.stack([r[b]["sp"] for b in range(NCORES)])[None],
            np.stack([r[b]["ks"] for b in range(NCORES)])[None],
            np.stack([r[b]["vs"] for b in range(NCORES)])[None],
            np.stack([r[b]["ss"] for b in range(NCORES)])[None])
```

```python
import contextlib
import os
import numpy as np
import ml_dtypes
import concourse.bass as bass
import concourse.mybir as mybir
from concourse.bass_utils import run_bass_kernel_spmd

F32 = mybir.dt.float32
BF16 = mybir.dt.bfloat16
ALU = mybir.AluOpType
AF = mybir.ActivationFunctionType

NCORES = 8
D = 1024
T = 2048
TS = 16
TT = T + TS
PAST = 1024
ALPHA = 2.0 ** 0.25
LN_EPS = 1e-5
RMS_EPS = 1e-6
NJUNK = 0
STRICT_SYNC = os.environ.get("KSTRICT", "1") == "1"
ENGS = ("pe", "act", "dve", "pool", "sp")


class Buf:
    __slots__ = ("name", "last_w", "rd_eng", "rd_dma")

    def __init__(self, name):
        self.name = name
        self.last_w = None
        self.rd_eng = {}
        self.rd_dma = []


class Op:
    __slots__ = ("eng", "fn", "deps", "sig", "dma", "need", "inc")

    def __init__(self, eng, fn, dma):
        self.eng = eng
        self.fn = fn
        self.deps = {}
        self.sig = None
        self.dma = dma
        self.need = False
        self.inc = 1


class Prog:
    def __init__(self, nc):
        self.nc = nc
        self.ops = []
        self.dma_keys = {}

    def add(self, eng, fn, reads=(), writes=(), dma_key=None):
        op = Op(eng, fn, dma_key is not None)
        for b in reads:
            if b.last_w is not None:
                self._dep(op, b.last_w, True)
        for b in writes:
            if b.last_w is not None:
                self._dep(op, b.last_w, False)
            for r in b.rd_eng.values():
                self._dep(op, r, False)
            for r in b.rd_dma:
                self._dep(op, r, False)
        for b in writes:
            b.last_w = op
            b.rd_eng = {}
            b.rd_dma = []
        for b in reads:
            if b.last_w is op:
                continue
            if op.dma:
                b.rd_dma.append(op)
            else:
                b.rd_eng[eng] = op
        if dma_key is not None:
            ent = self.dma_keys.setdefault(dma_key, [len(self.dma_keys), 0])
            ent[1] += 16
            op.sig = (("dma", ent[0]), ent[1])
            op.inc = 16
        self.ops.append(op)
        return op

    def _dep(self, op, d, raw):
        if d is op:
            return
        if d.eng == op.eng and not d.dma and not op.dma:
            if op.eng == "pe" or (not raw and not STRICT_SYNC):
                return
        op.deps[d] = True
        d.need = True

    def emit(self, final_eng="sp"):
        nc = self.nc
        cnt = {e: 0 for e in ENGS}
        for op in self.ops:
            if not op.dma and op.need:
                cnt[op.eng] += 1
                op.sig = (("eng", op.eng), cnt[op.eng])
        assert len(self.dma_keys) + len(ENGS) <= 100, len(self.dma_keys)
        with contextlib.ExitStack() as stack:
            sems = {}
            for e in ENGS:
                sems[("eng", e)] = stack.enter_context(nc.semaphore("s_" + e))
            for k, (i, _) in self.dma_keys.items():
                sems[("dma", i)] = stack.enter_context(nc.semaphore("d_%d" % i))
            per_eng = {e: [o for o in self.ops if o.eng == e] for e in ENGS}
            finals = [(("dma", i), v) for (i, v) in self.dma_keys.values()]
            know = {e: {} for e in ENGS}
            waits = {}
            wm = {}
            for op in self.ops:
                k = know[op.eng]
                need = {}
                for d in op.deps:
                    s, v = d.sig
                    if need.get(s, (0, None))[0] < v:
                        need[s] = (v, d)
                wl = []
                for s, (v, d) in need.items():
                    if k.get(s, 0) >= v:
                        continue
                    wl.append((s, v))
                    k[s] = v
                    for s2, v2 in wm.get(id(d), {}).items():
                        if k.get(s2, 0) < v2:
                            k[s2] = v2
                waits[id(op)] = wl
                if op.sig is not None and not op.dma:
                    snap = dict(k)
                    snap[op.sig[0]] = op.sig[1]
                    wm[id(op)] = snap
                elif op.sig is not None:
                    snap = dict(k)
                    snap[op.sig[0]] = op.sig[1]
                    wm[id(op)] = snap

            def run(e, engine):
                for op in per_eng[e]:
                    for s, v in waits[id(op)]:
                        engine.wait_ge(sems[s], v)
                    ins = op.fn(engine)
                    if op.sig is not None:
                        ins.then_inc(sems[op.sig[0]], op.inc)
                if e == final_eng:
                    for s, v in finals:
                        engine.wait_ge(sems[s], v)

            with nc.Block() as block:
                block.tensor(lambda eng: run("pe", eng))
                block.scalar(lambda eng: run("act", eng))
                block.vector(lambda eng: run("dve", eng))
                block.gpsimd(lambda eng: run("pool", eng))
                block.sync(lambda eng: run("sp", eng))


def build_program():
    nc = bass.Bass("TRN2", target_bir_lowering=False)
    P = Prog(nc)
    es = contextlib.ExitStack()

    def din(name, shape, dt=F32):
        return nc.dram_tensor(name, shape, dt, kind="ExternalInput").ap()

    def dout(name, shape, dt=F32):
        return nc.dram_tensor(name, shape, dt, kind="ExternalOutput").ap()

    xT_d = din("xT", [D, TT])
    xtok_d = din("xtok", [TT, D])
    ckT_d = din("ckT", [4, 128, PAST])
    cv_d = din("cv", [PAST, 512])
    st_d = din("state", [4, 128, 128])
    win_d = din("w_in", [D, 4096])
    wout_d = din("w_out", [D, D])
    lbl_d = din("lbl", [128, 8])
    ng_d = din("ng", [128, 4])
    lng_d = din("ln_g", [1, D])
    lnb_d = din("ln_b", [1, D])
    cbf_d = din("cbf", [128, 640], BF16)
    cf_d = din("cf", [128, 256])
    y_d = dout("y", [TT, D])
    kp_d = dout("kp", [8, T, 64])
    vp_d = dout("vp", [8, T, 64])
    sp_d = dout("sp", [4, 128, 128])
    ks_d = dout("ks", [8, TS, 64])
    vs_d = dout("vs", [8, TS, 64])
    ss_d = dout("ss", [4, 128, 128])

    AW = 53080
    AR = es.enter_context(nc.sbuf_tensor("arena", [128, AW], F32))
    cur = [0]

    def carve(words):
        o = cur[0]
        cur[0] += words
        assert cur[0] <= AW, cur[0]
        return o

    def f32v(off, n):
        return AR[:, off:off + n]

    def bf16v(off, nwords):
        return AR[:, off:off + nwords].bitcast(BF16)

    o_cbf = carve(320); CBF = bf16v(o_cbf, 320)
    IDENT = CBF[:, 0:128]; TRI = CBF[:, 128:256]; TRIC = CBF[:, 256:384]; ONES = CBF[:, 384:512]
    NEGM = CBF[:, 512:640]
    o_cf = carve(256); CF = f32v(o_cf, 256)
    MASKA = CF[:, 0:128]; MASKH = CF[:, 128:256]
    o_rs = carve(512); RS = f32v(o_rs, 512)
    o_sm = carve(64); SM = f32v(o_sm, 64)
    LBL = SM[:, 0:8]; NG = SM[:, 8:12]; LB = SM[:, 12:16]; A1 = SM[:, 16:20]; A0 = SM[:, 20:24]
    o_mixt = carve(8 * TT // 2); MIXT = bf16v(o_mixt, 8 * TT // 2).rearrange("p (e t) -> p e t", e=8)
    W0 = MIXT[:, 7, 0:2048].rearrange("p (k c) -> p k c", k=8)
    o_wb = carve(4096); WB = bf16v(o_wb, 4096).rearrange("p (s k c) -> p s k c", s=2, k=8)
    o_qs = carve(96); QS = bf16v(o_qs, 64).rearrange("p (h t) -> p h t", h=8); GS = bf16v(o_qs + 64, 32).rearrange("p (a t) -> p a t", a=4)
    o_l4 = carve(80); L4T = [f32v(o_l4 + i * 16, 16) for i in range(5)]
    o_sfp = carve(512); SFP = [f32v(o_sfp + i * 128, 128) for i in range(4)]
    o_rt2 = carve(1024); RT2 = [f32v(o_rt2 + i * 512, 512) for i in range(2)]
    o_i = carve(17 * 256); ITOK = bf16v(o_i, 17 * 256).rearrange("p (t c) -> p t c", t=17)
    o_x = carve(8 * TT // 2); XT = bf16v(o_x, 8 * TT // 2).rearrange("p (k t) -> p k t", k=8)
    XTOK = [f32v(o_x + i * 1024, 1024) for i in range(2)]
    RR = [f32v(o_x + 2048 + i * 1024, 1024) for i in range(2)]
    YY = [f32v(o_x + 4096 + i * 1024, 1024) for i in range(2)]
    o_yy2 = carve(2048)
    YY += [f32v(o_yy2 + i * 1024, 1024) for i in range(2)]
    LNG = f32v(o_x + 6144, 1024)
    LNB = f32v(o_x + 7168, 1024)
    RKW = 18800
    o_k = carve(RKW)
    KT = bf16v(o_k, 4096).rearrange("p (a t) -> p a t", a=4)
    V = bf16v(o_k + 4096, 4096).rearrange("p (t c) -> p t c", t=16)
    KTS = bf16v(o_k + 8192, 2080).rearrange("p (a t) -> p a t", a=4)
    VS = bf16v(o_k + 10272, 2304).rearrange("p (t c) -> p t c", t=9)
    QTZ = [[bf16v(o_k + 12576 + (2 * j + i) * 1032, 1032) for i in range(2)] for j in range(2)]
    GT = [bf16v(o_k + 16704 + j * 1032, 1032) for j in range(2)]
    assert 16704 + 2 * 1032 <= RKW
    hb = o_k
    QH = f32v(hb, TT); OH = QH; hb += TT
    FF = f32v(hb, TT); hb += TT
    SG = f32v(hb, TT); hb += TT
    QE = bf16v(hb, 1032); hb += 1032
    KE = bf16v(hb, 1032); hb += 1032
    KD = bf16v(hb, 1032); hb += 1032
    T2 = [f32v(hb + i * 512, 512) for i in range(2)]; hb += 1024
    GG = f32v(hb, 512); hb += 512
    KK = f32v(hb, 512); hb += 512
    BC = f32v(hb, 512); hb += 512
    EB = f32v(hb, 512); hb += 512
    ENB = f32v(hb, 512); hb += 512
    RT = [GG, KK]
    STM = [bf16v(hb + i * 64, 64) for i in range(2)]; hb += 128
    KETZ = [[bf16v(hb + (2 * i + c) * 64, 64) for c in range(2)] for i in range(2)]; hb += 256
    SBF = bf16v(hb, 33 * 64).rearrange("p (c v) -> p c v", c=33); hb += 33 * 64
    EBL = f32v(hb, 40); hb += 40
    SQ = bf16v(hb, 256); hb += 256
    assert hb <= o_k + RKW, hb - o_k
    o_w = carve(4096)

    def dbl(ap):
        return ap.rearrange("p (h n) -> p h n", h=2)

    E1 = [dbl(bf16v(o_w + i * 512, 512)) for i in range(2)]
    SPB = [dbl(bf16v(o_w + 1024 + i * 512, 512)) for i in range(3)]
    E2 = dbl(bf16v(o_w + 2560, 512))
    AB = [dbl(bf16v(o_w + 3072 + i * 512, 512)) for i in range(2)]
    KF = [f32v(o_w + i * 512, 512) for i in range(2)]
    VF = [f32v(o_w + 1024 + i * 512, 512) for i in range(2)]
    KBF = [bf16v(o_w + 2048 + i * 256, 256) for i in range(2)]

    ZZ = es.enter_context(nc.psum_tensor("zz", [128, 2, 512], F32))
    CC = es.enter_context(nc.psum_tensor("cc", [128, 2, 512], F32))
    OO = es.enter_context(nc.psum_tensor("oo", [128, 2, 512], F32))
    PSG = es.enter_context(nc.psum_tensor("psg", [128, 512], F32))
    PT = es.enter_context(nc.psum_tensor("pt", [128, 1024], BF16))
    PS = [ZZ[:, 0, :], ZZ[:, 1, :], CC[:, 0, :], CC[:, 1, :], OO[:, 0, :], OO[:, 1, :], PSG[:, :]]

    bCONST = Buf("const")
    bWB = [Buf("wb%d" % i) for i in range(2)]
    bXTc = [[Buf("xt%d_%d" % (c, i)) for i in range(8)] for c in range(4)]
    bXT = [b for c in bXTc for b in c]

    def xb(kc, c0_, n_):
        lo = min(c0_ // 512, 3); hi = min((c0_ + n_ - 1) // 512, 3)
        return [bXTc[c][kc] for c in range(lo, hi + 1)]
    bPS = [Buf("ps%d" % i) for i in range(7)]
    _bpt = Buf("pt")
    bPT = [_bpt, _bpt]
    bI = [Buf("i%d" % i) for i in range(17)]
    bKT = [Buf("kt%d" % i) for i in range(16)]
    bV = [Buf("v%d" % i) for i in range(16)]
    bKTS = Buf("kts"); bVS = Buf("vs")
    bKTSn = Buf("ktsn"); bVSn = Buf("vsn")
    bKF = [Buf("kf%d" % i) for i in range(2)]
    bVF = [Buf("vf%d" % i) for i in range(2)]
    bKBF = [Buf("kbf%d" % i) for i in range(2)]
    bQTs = [Buf("qt0"), Buf("qt1")]; bGTs = [Buf("gt0"), Buf("gt1")]
    bE1 = [Buf("e1_%d" % i) for i in range(2)]
    bSPB = [Buf("sp_%d" % i) for i in range(3)]
    bE2 = Buf("e2")
    bAB = [Buf("a_%d" % i) for i in range(2)]
    bMIX = [Buf("mix%d" % i) for i in range(8)]
    bQH = Buf("qh"); bOH = bQH; bFF = Buf("ff"); bSG = Buf("sg")
    bQE = Buf("qe"); bKE = Buf("ke"); bKD = Buf("kd")
    bT2 = [Buf("t2_%d" % i) for i in range(2)]
    bGG = Buf("gg"); bKK = Buf("kk"); bBC = Buf("bc"); bEB = Buf("eb"); bENB = Buf("enb")
    bRT = [bGG, bKK]
    bSTM = [Buf("stm%d" % i) for i in range(2)]
    bKETZ = [Buf("ketz%d" % i) for i in range(2)]
    bSFP = [Buf("sf%d" % i) for i in range(4)]; bSBF = [Buf("sbf%d" % i) for i in range(33)]
    bEBL = Buf("ebl"); bSQ = Buf("sq")
    bSM = Buf("sm"); bRS = Buf("rs")
    bXTOK = [Buf("xtok%d" % i) for i in range(2)]
    bRR = [Buf("rr%d" % i) for i in range(2)]
    bYY = [Buf("yy%d" % i) for i in range(4)]
    bMV = [Buf("mv%d" % i) for i in range(2)]
    bLN = Buf("lngb")
    bQS = Buf("qs")
    bL4 = [Buf("l4_%d" % i) for i in range(5)]
    bRT2 = [Buf("rt2_0"), Buf("rt2_1")]
    bJUNK = Buf("junk")
    regK_attn = bKT + bV + [bKTS, bVS, bKTSn, bVSn] + bQTs + bGTs
    regK_hgrn = ([bQH, bFF, bSG, bQE, bKE, bKD] + bT2 + [bGG, bKK, bBC, bEB, bENB] + bSTM + bKETZ + bSFP
                 + bSBF + [bEBL, bSQ])
    regX_c = bXTOK + bRR + bYY + [bLN]
    regW_p1 = bKF + bVF + bKBF
    regW_at = bE1 + bSPB + [bE2] + bAB

    def op(eng, method, reads, writes, **kw):
        return P.add(eng, lambda e: getattr(e, method)(**kw), reads, writes)

    def dma(eng, out, in_, reads, writes, key):
        return P.add(eng, lambda e: e.dma_start(out=out, in_=in_), reads, writes, dma_key=key)

    def mm(out, lhsT, rhs, start, stop, reads, writes, **kw):
        return P.add("pe", lambda e: e.matmul(out, lhsT=lhsT, rhs=rhs, start=start, stop=stop, **kw),
                     reads, writes)

    def act(out, in_, func, reads, writes, scale=1.0, bias=0.0):
        return P.add("act", lambda e: e.activation(out=out, in_=in_, func=func, bias=bias, scale=scale),
                     reads, writes)

    def alias(new_bufs, old_bufs):
        olds = []
        seen = set()
        for b in old_bufs:
            cands = list(b.rd_eng.values()) + list(b.rd_dma) + ([b.last_w] if b.last_w is not None else [])
            for o in cands:
                if id(o) not in seen:
                    seen.add(id(o))
                    olds.append(o)
        for nb in new_bufs:
            nb.last_w = None
            nb.rd_eng = {}
            nb.rd_dma = list(olds)

    def finish():
        with nc.allow_low_precision("bf16 matmul operands, fp32 accumulation"):
            P.emit()
        es.close()
        return nc

    dma("sp", CBF, cbf_d[:, :], [], [bCONST], "const")
    dma("sp", CF, cf_d[:, :], [], [bCONST], "const")
    dma("sp", LBL, lbl_d[:, :], [], [bCONST], "const")
    dma("sp", NG, ng_d[:, :], [], [bCONST], "const")
    jobs = [(win_d, 0, 512), (win_d, 512, 512)]
    jobs += [(win_d, 1536 + pr * 256, 256) for pr in range(4)]
    jobs += [(win_d, 1024, 512)]
    jobs += [(win_d, 2560 + hd * 384, 384) for hd in range(4)]
    jobs += [(wout_d, 0, 512), (wout_d, 512, 512)]

    loaded = set()

    def load_job(j):
        if j >= len(jobs) or j in loaded:
            return
        loaded.add(j)
        dram, c0, ncols = jobs[j]
        slot = j % 2
        done = 0
        while done < ncols:
            n = min(256, ncols - done)
            dma("pool", WB[:, slot, :, done:done + n],
                dram.rearrange("(k p) c -> p k c", p=128)[:, :, c0 + done:c0 + done + n], [], [bWB[slot]],
                ("wb", slot))
            done += n

    XCB = [(0, 512), (512, 512), (1024, 512), (1536, 528)]
    load_job(0)
    for cb, (c0_, n_) in enumerate(XCB):
        for kc in range(8):
            dma("pool", XT[:, kc, c0_:c0_ + n_], xT_d[kc * 128:(kc + 1) * 128, c0_:c0_ + n_], [], [bXTc[cb][kc]],
                ("xt", cb, kc))
        if cb == 1:
            load_job(1)
            loaded.add(2)
            dma("pool", W0, win_d.rearrange("(k p) c -> p k c", p=128)[:, :, 1536:1792], [], [bMIX[7]], "w0")
    op("pool", "memset", [], [bRS], ap=RS, constant=1.0)
    op("pool", "memset", [bRS], [bRS], ap=RS.rearrange("p (c t) -> p c t", t=64)[:, :, 0:1], constant=0.0)
    for j in range(2):
        for i in range(2):
            op("pool", "memset", [], [bQTs[j]], ap=QTZ[j][i], constant=0.0)
    op("pool", "memset", [], [bQS], ap=QS, constant=0.0)
    op("dve", "tensor_tensor", [bCONST], [bSM], out=A1, in0=LBL[:, 0:4], in1=LBL[:, 4:8], op=ALU.subtract)
    act(A1, A1, AF.Exp, [bSM], [bSM], scale=-1.0)
    op("dve", "tensor_scalar_add", [bSM], [bSM], out=A1, in0=A1, scalar1=1.0)
    op("dve", "reciprocal", [bSM], [bSM], out=LB, in_=A1)
    op("dve", "tensor_scalar", [bSM], [bSM], out=A1, in0=LB, scalar1=-0.5, scalar2=0.5, op0=ALU.mult, op1=ALU.add)
    op("dve", "tensor_scalar", [bSM], [bSM], out=A0, in0=LB, scalar1=0.5, scalar2=0.5, op0=ALU.mult, op1=ALU.add)
    def tok(tt):
        return (tt * 128, 128) if tt < 16 else (T, TS)

    def k_transposes(tt):
        t0, nt = tok(tt)
        s2 = tt % 2
        for j in range(4):
            P.add("pe", lambda e, j=j, s2=s2, nt=nt: e.transpose(
                out=PT[:, s2 * 512 + j * 128:s2 * 512 + j * 128 + nt], in_=KBF[s2][0:nt, j * 128:(j + 1) * 128],
                identity=IDENT[0:nt, 0:nt]), [bKBF[s2], bCONST], [bPT[s2]])
        src = PT[:, s2 * 512:(s2 + 1) * 512].rearrange("p (a t) -> p a t", a=4)[:, :, 0:nt]
        if tt < 16:
            op("dve", "tensor_copy", [bPT[s2]], [bKT[tt]], out=KT[:, :, t0:t0 + nt], in_=src)
        else:
            op("dve", "tensor_copy", [bPT[s2]], [bKTSn], out=KTS[:, :, PAST:PAST + nt], in_=src)

    def proj_tokmajor(g):
        jidx = g if g < 2 else 6
        slot = jidx % 2
        if jidx >= 2:
            load_job(jidx + 1)
        for tt in range(17):
            t0, nt = tok(tt)
            bi = (tt % 4) if g < 2 else 6
            bank = PS[bi]; bb = bPS[bi]
            for kc in range(8):
                mm(bank[0:nt, :], XT[:, kc, t0:t0 + nt], WB[:, slot, kc, :], kc == 0, kc == 7,
                   xb(kc, t0, nt) + [bWB[slot]], [bb])
                if kc == 3 and g == 2:
                    yield
            s2 = tt % 2
            if g == 0:
                op("dve", "tensor_copy", [bb], [bKF[s2]], out=KF[s2][0:nt, :], in_=bank[0:nt, :])
                dst = (kp_d[:, t0:t0 + nt, :] if tt < 16 else ks_d[:, :, :]).rearrange("h t d -> t h d")
                dma("sp", dst, KF[s2][0:nt, :].rearrange("t (h d) -> t h d", h=8), [bKF[s2]], [], ("kf", s2))
                if tt < 16:
                    act(KBF[s2][0:nt, :], KF[s2][0:nt, :], AF.Identity, [bKF[s2]], [bKBF[s2]])
                else:
                    op("pool", "tensor_copy", [bKF[s2]], [bKBF[s2]], out=KBF[s2][0:nt, :], in_=KF[s2][0:nt, :])
                if tt >= 1:
                    k_transposes(tt - 1)
                if tt == 16:
                    k_transposes(16)
            elif g == 1:
                op("dve", "tensor_copy", [bb], [bVF[s2]], out=VF[s2][0:nt, :], in_=bank[0:nt, :])
                dst = (vp_d[:, t0:t0 + nt, :] if tt < 16 else vs_d[:, :, :]).rearrange("h t d -> t h d")
                dma("sp", dst, VF[s2][0:nt, :].rearrange("t (h d) -> t h d", h=8), [bVF[s2]], [], ("vf", s2))
                if tt < 16:
                    act(V[0:nt, tt, :], VF[s2][0:nt, :], AF.Identity, [bVF[s2]], [bV[tt]])
                else:
                    op("pool", "tensor_copy", [bVF[s2]], [bVSn], out=VS[0:nt, 8, :], in_=VF[s2][0:nt, :])
            else:
                op("dve", "tensor_copy", [bb], [bI[tt]], out=ITOK[0:nt, tt, :], in_=bank[0:nt, :])
            yield

    col_tiles = [(i * 512, 512) for i in range(4)] + [(T, TS)]
    H0 = slice(0, 64); H1 = slice(64, 128)

    def inproj_pair(pr):
        jidx = 2 + pr
        slot = jidx % 2
        jb = pr % 2
        if pr > 0:
            load_job(jidx + 1)
        wq_ = W0[:, :, 0:128] if pr == 0 else WB[:, slot, :, 0:128]
        wg_ = W0[:, :, 128:256] if pr == 0 else WB[:, slot, :, 128:256]
        bw_ = bMIX[7] if pr == 0 else bWB[slot]
        for ci, (c0, n) in enumerate(col_tiles):
            bank = PS[6]; bb = bPS[6]
            for kc in range(8):
                mm(bank[:, 0:n], wq_[:, kc, :], XT[:, kc, c0:c0 + n], kc == 0, kc == 7,
                   xb(kc, c0, n) + [bw_], [bb])
                if kc == 3:
                    yield
            op("dve", "tensor_scalar_mul", [bb], [bQTs[jb]], out=QTZ[jb][0][H0, c0:c0 + n], in0=bank[H0, 0:n],
               scalar1=0.125)
            op("dve", "tensor_scalar_mul", [bb], [bQTs[jb]], out=QTZ[jb][1][H1, c0:c0 + n], in0=bank[H1, 0:n],
               scalar1=0.125)
            yield
            for kc in range(8):
                mm(bank[:, 0:n], wg_[:, kc, :], XT[:, kc, c0:c0 + n], kc == 0, kc == 7,
                   xb(kc, c0, n) + [bw_], [bb])
                if kc == 3:
                    yield
            if pr == 0:
                act(GT[jb][:, c0:c0 + n], bank[:, 0:n], AF.Silu, [bb], [bGTs[jb]])
            else:
                op("dve", "tensor_copy", [bb], [bGTs[jb]], out=GT[jb][:, c0:c0 + n], in_=bank[:, 0:n])
            yield
        if pr != 0:
            act(GT[jb][:, 0:TT], GT[jb][:, 0:TT], AF.Silu, [bGTs[jb]], [bGTs[jb]])
        for hh in range(2):
            op("pool", "tensor_copy", [bQTs[jb]], [bQS], out=QS[:, 2 * pr + hh, :], in_=QTZ[jb][hh][:, T:TT])
        op("pool", "tensor_copy", [bGTs[jb]], [bQS], out=GS[:, pr, :], in_=GT[jb][:, T:TT])
        yield

    gk = proj_tokmajor(0)
    gv = proj_tokmajor(1)
    gi = inproj_pair(0)
    for _ in range(8):
        next(gk)
    for _ in range(8):
        next(gv)
    for _ in range(8):
        next(gi)
    for blk in range(2, 5):
        ntl = 4 if blk < 4 else 1
        for _ in range(ntl):
            next(gk)
        for _ in range(ntl):
            next(gv)
        for _ in range(4):
            next(gi)
    for g_ in (gk, gv):
        for _ in g_:
            pass
    load_job(3)
    for _ in gi:
        pass

    if os.environ.get('KSTOP') == '1':
        return finish()
    alias(regW_at, regW_p1)
    for pr in range(4):
        dma("pool", KTS[:, pr, 0:PAST], ckT_d[pr, :, :], [], [bKTS], "kts")
    for hf in range(2):
        dma("pool", VS[:, hf * 4:(hf + 1) * 4, :],
            cv_d.rearrange("(k p) c -> p k c", p=128)[:, hf * 4:(hf + 1) * 4, :], [], [bVS], "vs")

    side = []

    def side_step():
        while side:
            try:
                next(side[0])
                return
            except StopIteration:
                side.pop(0)

    tiles = []
    for pr in range(4):
        pc = slice(pr * 128, (pr + 1) * 128)
        for Q in range(4):
            nb = 4 * Q + 4
            for bi, kb in enumerate(range(4 * Q + 3, -1, -1)):
                tiles.append(dict(pr=pr, jb=pr % 2, pstart=(Q == 0 and bi == 0), q0=512 * Q, N=512,
                                  kt=KT[:, pr, kb * 128:(kb + 1) * 128], v=V[:, kb, pc], nk=128,
                                  c0=max(0, 128 * (kb - 4 * Q)), diag=kb >= 4 * Q, mw=128, rd=[bKT[kb], bV[kb]],
                                  first=bi == 0, last=bi == nb - 1))
    nt_ = len(tiles)
    for i, t in enumerate(tiles):
        t["e1"] = i % 2; t["sp"] = i % 3; t["a"] = i % 2
    bZ = [bPS[0], bPS[1]]; bC = [bPS[2], bPS[3]]; bO = [bPS[4], bPS[5]]

    def st_z(t):
        nk = t["nk"]; c0 = t["c0"]; N = t["N"]; q0 = t["q0"]; jb = t["jb"]
        for hh in range(2):
            mm(ZZ[0:nk, hh, c0:N], t["kt"], QTZ[jb][hh][:, q0 + c0:q0 + N], True, not t["diag"],
               t["rd"] + [bQTs[jb]], [bZ[hh]], skip_group_check=True)
            if t["diag"]:
                mw = t["mw"]
                mm(ZZ[0:nk, hh, c0:c0 + mw], IDENT[0:nk, 0:nk], NEGM[0:nk, 0:mw], False, True, [bCONST], [bZ[hh]],
                   skip_group_check=True)
        e1 = E1[t["e1"]]; be1 = bE1[t["e1"]]
        act(e1[0:nk, :, c0:N], ZZ[0:nk, :, c0:N], AF.Exp, bZ, [be1])

    def st_ln(t):
        nk = t["nk"]; c0 = t["c0"]; N = t["N"]
        e1 = E1[t["e1"]]; be1 = bE1[t["e1"]]
        sp = SPB[t["sp"]]
        act(sp[0:nk, :, c0:N], e1[0:nk, :, c0:N], AF.Ln, [be1], [bSPB[t["sp"]]], bias=1.0)

    def st_tric(t):
        if t["last"]:
            return
        nk = t["nk"]; c0 = t["c0"]; N = t["N"]
        sp = SPB[t["sp"]]
        for hh in range(2):
            mm(CC[:, hh, c0:N], TRIC[0:nk, :], sp[0:nk, hh, c0:N], False, False, [bSPB[t["sp"]], bCONST],
               [bC[hh]], skip_group_check=True)

    def st_tri(t):
        nk = t["nk"]; c0 = t["c0"]; N = t["N"]
        sp = SPB[t["sp"]]
        for hh in range(2):
            mm(CC[:, hh, c0:N], TRI[0:nk, :], sp[0:nk, hh, c0:N], t["first"], False, [bSPB[t["sp"]], bCONST],
               [bC[hh]], skip_group_check=True)
        act(E2[0:nk, :, c0:N], CC[0:nk, :, c0:N], AF.Exp, bC, [bE2], scale=-1.0)

    def st_mul(t):
        nk = t["nk"]; c0 = t["c0"]; N = t["N"]
        a = AB[t["a"]]
        op("dve", "tensor_tensor", [bE1[t["e1"]], bE2], [bAB[t["a"]]], out=a[0:nk, :, c0:N],
           in0=E1[t["e1"]][0:nk, :, c0:N], in1=E2[0:nk, :, c0:N], op=ALU.mult)

    def st_av(t):
        nk = t["nk"]; c0 = t["c0"]; N = t["N"]; q0 = t["q0"]; jb = t["jb"]; pr = t["pr"]
        a = AB[t["a"]]
        for hh in range(2):
            mm(OO[:, hh, c0:N], t["v"], a[0:nk, hh, c0:N], t["first"], t["last"], t["rd"] + [bAB[t["a"]]],
               [bO[hh]], skip_group_check=True)
        if t["last"]:
            for hh, rows in ((0, H0), (1, H1)):
                op("dve", "tensor_tensor", [bO[hh], bGTs[jb]], [bMIX[pr]], out=MIXT[rows, pr, q0:q0 + N],
                   in0=OO[rows, hh, 0:N], in1=GT[jb][rows, q0:q0 + N], op=ALU.mult)

    for step in range(nt_ + 2):
        if step < nt_:
            t = tiles[step]
            if t["pstart"]:
                while side:
                    side_step()
                side.append(inproj_pair(t["pr"] + 1) if t["pr"] < 3 else proj_tokmajor(2))
            st_z(t)
        if 0 <= step - 2 < nt_:
            st_tric(tiles[step - 2])
        if 0 <= step - 1 < nt_:
            st_tri(tiles[step - 1])
        if step < nt_:
            st_ln(tiles[step])
        if 0 <= step - 1 < nt_:
            st_mul(tiles[step - 1])
        if 0 <= step - 2 < nt_:
            st_av(tiles[step - 2])
        side_step()
    while side:
        side_step()

    stiles = [dict(kts=lambda a: KTS[:, a, PAST:PAST + TS], vs=lambda a: VS[0:TS, 8, a * 128:(a + 1) * 128], nk=TS,
                   diag=True, rd=[bKTSn, bVSn], first=True, last=False)]
    for bi, kb in enumerate(range(7, -1, -1)):
        stiles.append(dict(kts=lambda a, kb=kb: KTS[:, a, kb * 128:(kb + 1) * 128],
                           vs=lambda a, kb=kb: VS[:, kb, a * 128:(a + 1) * 128], nk=128, diag=False,
                           rd=[bKTS, bVS], first=False, last=bi == 7))
    for i, t in enumerate(stiles):
        t["e1"] = i % 2; t["sp"] = i % 3; t["a"] = i % 2
    ZSb = ZZ[:, 0, 0:128]; CSb = CC[:, 0, 0:128]; OSb = OO[:, 0, 0:128]
    W8 = 8 * TS

    def ss_z(t):
        nk = t["nk"]
        for h in range(8):
            hs = slice(h * TS, (h + 1) * TS)
            mm(ZSb[0:nk, hs], t["kts"](h // 2), QS[:, h, :], h == 0, (not t["diag"]) and h == 7, t["rd"] + [bQS],
               [bPS[0]], skip_group_check=True)
            if t["diag"]:
                mm(ZSb[0:nk, hs], IDENT[0:nk, 0:nk], NEGM[0:nk, 0:TS], False, h == 7, [bCONST], [bPS[0]],
                   skip_group_check=True)
        act(E1[t["e1"]][0:nk, 0, 0:W8], ZSb[0:nk, :], AF.Exp, [bPS[0]], [bE1[t["e1"]]])

    def ss_ln(t):
        nk = t["nk"]
        act(SPB[t["sp"]][0:nk, 0, 0:W8], E1[t["e1"]][0:nk, 0, 0:W8], AF.Ln, [bE1[t["e1"]]], [bSPB[t["sp"]]], bias=1.0)

    def ss_tric(t):
        if t["last"]:
            return
        nk = t["nk"]
        mm(CSb, TRIC[0:nk, :], SPB[t["sp"]][0:nk, 0, 0:W8], False, False, [bSPB[t["sp"]], bCONST], [bPS[2]],
           skip_group_check=True)

    def ss_tri(t):
        nk = t["nk"]
        mm(CSb, TRI[0:nk, :], SPB[t["sp"]][0:nk, 0, 0:W8], t["first"], False, [bSPB[t["sp"]], bCONST], [bPS[2]],
           skip_group_check=True)
        act(E2[0:nk, 0, 0:W8], CSb[0:nk, :], AF.Exp, [bPS[2]], [bE2], scale=-1.0)

    def ss_mul(t):
        nk = t["nk"]
        op("dve", "tensor_tensor", [bE1[t["e1"]], bE2], [bAB[t["a"]]], out=AB[t["a"]][0:nk, 0, 0:W8],
           in0=E1[t["e1"]][0:nk, 0, 0:W8], in1=E2[0:nk, 0, 0:W8], op=ALU.mult)

    def ss_av(t):
        nk = t["nk"]
        for h in range(8):
            hs = slice(h * TS, (h + 1) * TS)
            mm(OSb[:, hs], t["vs"](h // 2), AB[t["a"]][0:nk, 0, hs], t["first"] and h == 0, t["last"] and h == 7,
               t["rd"] + [bAB[t["a"]]], [bPS[4]], skip_group_check=True)
        if t["last"]:
            for h in range(8):
                rows = H0 if h % 2 == 0 else H1
                op("dve", "tensor_tensor", [bPS[4], bQS], [bMIX[h // 2]], out=MIXT[rows, h // 2, T:TT],
                   in0=OSb[rows, h * TS:(h + 1) * TS], in1=GS[rows, h // 2, :], op=ALU.mult)

    ns_ = len(stiles)
    for step in range(ns_ + 2):
        if step < ns_:
            ss_z(stiles[step])
        if 0 <= step - 2 < ns_:
            ss_tric(stiles[step - 2])
        if 0 <= step - 1 < ns_:
            ss_tri(stiles[step - 1])
        if step < ns_:
            ss_ln(stiles[step])
        if 0 <= step - 1 < ns_:
            ss_mul(stiles[step - 1])
        if 0 <= step - 2 < ns_:
            ss_av(stiles[step - 2])

    if os.environ.get('KSTOP') == 'A':
        return finish()
    alias(regK_hgrn, regK_attn)
    bHprev = []
    bHprev2 = []
    for i in range(2):
        for c in range(2):
            op("pool", "memset", [], [bKETZ[i]], ap=KETZ[i][c], constant=0.0)
    NCT = len(col_tiles)

    def ctile(tt):
        return min(tt // 4, 4)

    for hd in range(4):
        slot = (7 + hd) % 2
        load_job(7 + hd + 1)
        wq = WB[:, slot, :, 0:128]; wf = WB[:, slot, :, 128:256]; wg = WB[:, slot, :, 256:384]
        bQHc = [Buf("qh%d" % i) for i in range(NCT)]
        bFFc = [Buf("ff%d" % i) for i in range(NCT)]
        bSGc = [Buf("sg%d" % i) for i in range(NCT)]
        bQEc = [Buf("qe%d" % i) for i in range(NCT)]
        bKEc = [Buf("ke%d" % i) for i in range(NCT)]
        bKDc = [Buf("kd%d" % i) for i in range(NCT)]
        bEBLc = [Buf("ebl%d" % i) for i in range(NCT)]
        alias(bQHc + bFFc + bSGc, [bQH, bFF, bSG] + bHprev)
        alias(bQEc + bKEc + bKDc + bEBLc, [bQE, bKE, bKD, bEBL] + bHprev2)
        bHprev = bQHc + bFFc + bSGc
        bHprev2 = bQEc + bKEc + bKDc + bEBLc
        bi_ = 0
        for ci, (c0, n) in enumerate(col_tiles):
            cs = slice(c0, c0 + n)
            for (w, kind) in ((wq, "q"), (wf, "f"), (wg, "g")):
                bank = PS[bi_ % 4]; bb = bPS[bi_ % 4]; bi_ += 1
                for kc in range(8):
                    mm(bank[:, 0:n], w[:, kc, :], XT[:, kc, cs], kc == 0, kc == 7, xb(kc, c0, n) + [bWB[slot]], [bb])
                if kind == "q":
                    act(QH[:, cs], bank[:, 0:n], AF.Silu, [bb], [bQHc[ci]])
                elif kind == "g":
                    act(SG[:, cs], bank[:, 0:n], AF.Silu, [bb], [bSGc[ci]])
                else:
                    t2 = T2[ci % 2]; bt2 = bT2[ci % 2]
                    act(t2[:, 0:n], bank[:, 0:n], AF.Tanh, [bb], [bt2], scale=0.5)
                    op("dve", "tensor_scalar", [bt2, bSM], [bFFc[ci]], out=FF[:, cs], in0=t2[:, 0:n],
                       scalar1=A1[:, hd:hd + 1], scalar2=A0[:, hd:hd + 1], op0=ALU.mult, op1=ALU.add)

        hv = slice(hd * 128, (hd + 1) * 128)
        if hd == 3:
            alias(regX_c, bXT)
            load_job(12)
            dma("sp", LNG, lng_d[0:1, :].partition_broadcast(128), [], [bLN], "lngb")
            dma("sp", LNB, lnb_d[0:1, :].partition_broadcast(128), [], [bLN], "lngb")
            for tt_ in range(2):
                t0_, nt_c = tok(tt_)
                dma("sp", XTOK[tt_][0:nt_c, :], xtok_d[t0_:t0_ + nt_c, :], [], [bXTOK[tt_]], ("xtok", tt_))

        GG_, KK_, BC_, EB_, ENB_ = GG, KK, BC, EB, ENB
        bGG_, bKK_, bBC_, bEB_, bENB_ = bGG, bKK, bBC, bEB, bENB

        def Lst(ci):
            c0, n = col_tiles[ci]
            cs = slice(c0, c0 + n)
            if ci == 4:
                GG, KK, BC, EB, ENB = L4T
                bGG, bKK, bBC, bEB, bENB = bL4
            else:
                GG, KK, BC, EB, ENB = GG_, KK_, BC_, EB_, ENB_
                bGG, bKK, bBC, bEB, bENB = bGG_, bKK_, bBC_, bEB_, bENB_

            def l1():
                act(GG[:, 0:n], FF[:, cs], AF.Ln, [bFFc[ci]], [bGG])
                act(KK[:, 0:n], FF[:, cs], AF.Identity, [bFFc[ci]], [bKK], scale=-1.0, bias=1.0)

            def l2():
                op("dve", "tensor_tensor_scan", [bRS, bGG], [bBC], out=BC[:, 0:n], data0=RS[:, 0:n], data1=GG[:, 0:n],
                   initial=0.0, op0=ALU.mult, op1=ALU.add)

            def l3():
                act(EB[:, 0:n], BC[:, 0:n], AF.Exp, [bBC], [bEB])
                act(ENB[:, 0:n], BC[:, 0:n], AF.Exp, [bBC], [bENB], scale=-1.0)

            def l4():
                op("pool", "tensor_tensor", [bQHc[ci], bEB], [bQEc[ci]], out=QE[:, cs], in0=QH[:, cs], in1=EB[:, 0:n],
                   op=ALU.mult)
                op("dve", "tensor_tensor", [bKK, bENB], [bKEc[ci]], out=KE[:, cs], in0=KK[:, 0:n], in1=ENB[:, 0:n],
                   op=ALU.mult)
                if n == 512:
                    op("pool", "tensor_copy", [bEB], [bEBLc[ci]], out=EBL[:, ci * 8:ci * 8 + 8],
                       in_=EB.rearrange("p (c t) -> p c t", t=64)[:, :, 63])
                else:
                    op("pool", "tensor_copy", [bEB], [bEBLc[ci]], out=EBL[:, 32:33], in_=EB[:, n - 1:n])

            def l5():
                if n == 512:
                    op("dve", "tensor_tensor", [bKEc[ci], bEBLc[ci]], [bKDc[ci]],
                       out=KD[:, cs].rearrange("p (c t) -> p c t", t=64),
                       in0=KE[:, cs].rearrange("p (c t) -> p c t", t=64),
                       in1=EBL[:, ci * 8:ci * 8 + 8].unsqueeze(2).to_broadcast([128, 8, 64]), op=ALU.mult)
                else:
                    op("pool", "tensor_scalar", [bKEc[ci], bEBLc[ci]], [bKDc[ci]], out=KD[:, cs], in0=KE[:, cs],
                       scalar1=EBL[:, 32:33], scalar2=None, op0=ALU.mult)

            return [l1, l2, l3, l4, l5]

        def Mst(ci):
            c0, n = col_tiles[ci]
            cs = slice(c0, c0 + n)
            bank = PS[ci % 2]; bb = bPS[ci % 2]
            r0 = RT2[0]; r1 = RT2[1]

            def m1():
                act(SQ[:, 0:n], OH[:, cs], AF.Square, [bQHc[ci]], [bSQ])

            def m2():
                mm(bank[:, 0:n], ONES, SQ[:, 0:n], True, True, [bSQ, bCONST], [bb])
                act(r0[:, 0:n], bank[:, 0:n], AF.Ln, [bb], [bRT2[0]], scale=1.0 / 128.0, bias=RMS_EPS)

            def m3():
                act(r0[:, 0:n], r0[:, 0:n], AF.Exp, [bRT2[0]], [bRT2[0]], scale=-0.5)

            def m4():
                op("dve", "tensor_tensor", [bQHc[ci], bRT2[0]], [bRT2[1]], out=r1[:, 0:n], in0=OH[:, cs],
                   in1=r0[:, 0:n], op=ALU.mult)
                op("dve", "scalar_tensor_tensor", [bRT2[1], bCONST, bSGc[ci]], [bMIX[4 + hd]],
                   out=MIXT[:, 4 + hd, cs], in0=r1[:, 0:n], scalar=NG[:, hd:hd + 1], in1=SG[:, cs], op0=ALU.mult,
                   op1=ALU.mult)

            return [m1, m2, m3, m4]

        def mk_recur(first_zero):
            st = dict(ch2=0, ch3=0, par=spar[0])

            def p2a(tt):
                t0, nt = tok(tt)
                k2 = tt % 2
                ci = ctile(tt)
                P.add("pe", lambda e, k2=k2, t0=t0, nt=nt: e.transpose(
                    out=PT[0:nt, k2 * 512:k2 * 512 + 128], in_=KD[:, t0:t0 + nt], identity=IDENT[:, :]),
                    [bKDc[ci], bCONST], [bPT[k2]])
                nch = max(1, nt // 64)
                for c in range(nch):
                    cl = min(64, nt)
                    rs_ = slice(c * 64, c * 64 + cl)
                    if nt == 128:
                        act(KETZ[k2][c][rs_, :], PT[rs_, k2 * 512:k2 * 512 + 128], AF.Identity, [bPT[k2]], [bKETZ[k2]])
                    else:
                        op("dve", "tensor_copy", [bPT[k2]], [bKETZ[k2]], out=KETZ[k2][c][rs_, :],
                           in_=PT[rs_, k2 * 512:k2 * 512 + 128])

            def p2b(tt):
                t0, nt = tok(tt)
                k2 = tt % 2
                ci = ctile(tt)
                nch = max(1, nt // 64)
                for c in range(nch):
                    ch = st["ch2"]
                    U = PS[3 + ch % 2]; bU = bPS[3 + ch % 2]
                    if nt == 128:
                        mm(U[:, 0:128], KETZ[k2][c][:, :], ITOK[:, tt, hv], True, True, [bKETZ[k2], bI[tt]], [bU])
                    else:
                        mm(U[:, 0:128], KETZ[k2][c][0:nt, :], ITOK[0:nt, tt, hv], True, True, [bKETZ[k2], bI[tt]], [bU])
                    ebc = EBL[:, (tt * 2 + c):(tt * 2 + c) + 1] if tt < 16 else EBL[:, 32:33]
                    so = st["par"]; sn = (so + 1) % 4
                    if first_zero and ch == 0:
                        op("dve", "tensor_copy", [bU], [bSFP[sn]], out=SFP[sn], in_=U[:, 0:128])
                    else:
                        op("dve", "scalar_tensor_tensor", [bU, bEBLc[ci], bSFP[so]], [bSFP[sn]], out=SFP[sn],
                           in0=SFP[so], scalar=ebc, in1=U[:, 0:128], op0=ALU.mult, op1=ALU.add)
                    st["par"] = sn
                    st["ch2"] = ch + 1
                    op("pool", "tensor_copy", [bSFP[sn]], [bSBF[ch + 1]], out=SBF[:, ch + 1, :], in_=SFP[sn])

            def p3a(tt):
                t0, nt = tok(tt)
                k2 = tt % 2
                ci = ctile(tt)
                Sb = PS[2]; bSb = bPS[2]
                mm(Sb[0:nt, 0:nt], KE[:, t0:t0 + nt], QE[:, t0:t0 + nt], True, True, [bKEc[ci], bQEc[ci]], [bSb])
                op("dve", "tensor_tensor", [bSb, bCONST], [bSTM[k2]], out=STM[k2][0:nt, 0:nt], in0=Sb[0:nt, 0:nt],
                   in1=MASKH[0:nt, 0:nt], op=ALU.mult)

            def p3(tt):
                t0, nt = tok(tt)
                k2 = tt % 2
                ci = ctile(tt)
                ch = st["ch3"]
                Ob = PS[5 + k2]; bOb = bPS[5 + k2]
                nch = max(1, nt // 64)
                terms = [c for c in range(nch) if not (first_zero and ch + c == 0)]
                mm(Ob[:, 0:nt], ITOK[0:nt, tt, hv], STM[k2][0:nt, 0:nt], True, len(terms) == 0,
                   [bI[tt], bSTM[k2]], [bOb], skip_group_check=True)
                for c in terms:
                    cl = min(64, nt)
                    mm(Ob[:, c * 64:c * 64 + cl], SBF[:, ch + c, :], QE[:, t0 + c * 64:t0 + c * 64 + cl],
                       False, c == terms[-1], [bSBF[ch + c], bQEc[ci]], [bOb], skip_group_check=True)
                st["ch3"] = ch + nch
                act(OH[:, t0:t0 + nt], Ob[:, 0:nt], AF.Identity, [bOb], [bQHc[ci]])

            return p2a, p2b, p3a, p3, st

        spar = [0]
        sched = {}

        def at(it, fn):
            sched.setdefault(it, []).append(fn)

        dma("sp", SFP[0], st_d[hd, :, :], [], [bSFP[0]], ("sfi", 0))
        op("pool", "tensor_copy", [bSFP[0]], [bSBF[0]], out=SBF[:, 0, :], in_=SFP[0])
        for f0, f4 in zip(Lst(0), Lst(4)):
            f0()
            f4()
        p2a, p2b, p3a, p3, st = mk_recur(False)
        l1s = Lst(1)
        l1s[0]()
        p2a(16)
        l1s[1]()
        p3a(16)
        l1s[2]()
        p2b(16)
        l1s[3]()
        p3(16)
        l1s[4]()
        spar[0] = st["par"]
        dma("sp", ss_d[hd, :, :], SFP[spar[0]], [bSFP[spar[0]]], [], ("sfo", spar[0]))
        for ci in (2, 3):
            for k_, fn in enumerate(Lst(ci)):
                at(4 * (ci - 2) + k_, fn)
        for k_, fn in enumerate(Mst(4)):
            at(1 + k_, fn)
        for ci in (0, 1, 2, 3):
            for k_, fn in enumerate(Mst(ci)):
                at(4 * (ci + 1) + 1 + k_, fn)
        p2a, p2b, p3a, p3, st = mk_recur(True)
        p2a(0)
        for tt in range(16):
            p2b(tt)
            if tt + 1 < 16:
                p2a(tt + 1)
            if tt >= 1:
                p3(tt - 1)
            p3a(tt)
            for fn in sched.pop(tt, []):
                fn()
        p3(15)
        spar[0] = st["par"]
        dma("sp", sp_d[hd, :, :], SFP[spar[0]], [bSFP[spar[0]]], [], ("sfo", spar[0]))
        for it in sorted(sched):
            for fn in sched[it]:
                fn()

    if os.environ.get('KSTOP') == 'B':
        return finish()
    wslot = [11 % 2, 12 % 2]
    MV = SM[:, 24:56]
    def c_stage_a(tt):
        t0, nt = tok(tt)
        s2 = tt % 2
        if tt >= 2:
            dma("sp", XTOK[s2][0:nt, :], xtok_d[t0:t0 + nt, :], [], [bXTOK[s2]], ("xtok", s2))
        for h2 in range(2):
            bi = (tt * 2 + h2) % 4
            bank = PS[bi]; bb = bPS[bi]
            for ec in range(8):
                mm(bank[0:nt, :], MIXT[:, ec, t0:t0 + nt], WB[:, wslot[h2], ec, :], ec == 0, ec == 7,
                   [bMIX[ec], bWB[wslot[h2]]], [bb])
            op("dve", "scalar_tensor_tensor", [bXTOK[s2], bb], [bRR[s2]], out=RR[s2][0:nt, h2 * 512:(h2 + 1) * 512],
               in0=XTOK[s2][0:nt, h2 * 512:(h2 + 1) * 512], scalar=ALPHA, in1=bank[0:nt, :], op0=ALU.mult,
               op1=ALU.add)
        st6 = MV[:, s2 * 16:s2 * 16 + 12].rearrange("p (a b) -> p a b", a=2)
        mv = MV[:, s2 * 16 + 12:s2 * 16 + 14]
        rstd = MV[:, s2 * 16 + 14:s2 * 16 + 15]
        for h2 in range(2):
            op("dve", "bn_stats", [bRR[s2]], [bMV[s2]], out=st6[0:nt, h2, :], in_=RR[s2][0:nt, h2 * 512:(h2 + 1) * 512])
        op("dve", "bn_aggr", [bMV[s2]], [bMV[s2]], out=mv[0:nt, :], in_=st6[0:nt, :, :])
        act(rstd[0:nt, :], mv[0:nt, 1:2], AF.Ln, [bMV[s2]], [bMV[s2]], bias=LN_EPS)
        act(rstd[0:nt, :], rstd[0:nt, :], AF.Exp, [bMV[s2]], [bMV[s2]], scale=-0.5)
        if tt >= 1:
            c_gamma(tt - 1)
        nmr = MV[:, s2 * 16 + 15:s2 * 16 + 16]
        if nt == 128:
            op("dve", "scalar_tensor_tensor", [bMV[s2]], [bMV[s2]], out=nmr[0:nt, :], in0=mv[0:nt, 0:1], scalar=-1.0,
               in1=rstd[0:nt, :], op0=ALU.mult, op1=ALU.mult)
            P.add("act", lambda e, s2=s2, nt=nt, nmr=nmr, rstd=rstd, s4=tt % 4: e.activation(
                out=YY[s4][0:nt, :], in_=RR[s2][0:nt, :], func=AF.Identity, bias=nmr[0:nt, :], scale=rstd[0:nt, :]),
                [bRR[s2], bMV[s2]], [bYY[tt % 4]])
        else:
            op("dve", "tensor_scalar", [bRR[s2], bMV[s2]], [bYY[tt % 4]], out=YY[tt % 4][0:nt, :], in0=RR[s2][0:nt, :],
               scalar1=mv[0:nt, 0:1], scalar2=rstd[0:nt, :], op0=ALU.subtract, op1=ALU.mult)

    def c_gamma(tt):
        t0, nt = tok(tt)
        s2 = tt % 2
        op("dve", "tensor_tensor", [bYY[tt % 4], bLN], [bYY[tt % 4]], out=YY[tt % 4][0:nt, :], in0=YY[tt % 4][0:nt, :],
           in1=LNG[0:nt, :], op=ALU.mult)

    def c_stage_b(tt):
        t0, nt = tok(tt)
        s2 = tt % 2
        op("pool", "tensor_tensor", [bYY[tt % 4], bLN], [bYY[tt % 4]], out=YY[tt % 4][0:nt, :], in0=YY[tt % 4][0:nt, :],
           in1=LNB[0:nt, :], op=ALU.add)
        dma("pool", y_d[t0:t0 + nt, :], YY[tt % 4][0:nt, :], [bYY[tt % 4]], [], ("yy", tt % 4))

    for tt in range(18):
        if tt < 17:
            c_stage_a(tt)
        else:
            c_gamma(16)
        if tt >= 1:
            c_stage_b(tt - 1)

    return finish()


_NC_CACHE = {}


def _consts():
    bf = ml_dtypes.bfloat16
    j = np.arange(128)
    ident = np.eye(128, dtype=np.float32)
    tri = (j[:, None] >= j[None, :]).astype(np.float32)
    maska = (j[:, None] < j[None, :]).astype(np.float32)
    negm = (maska - 1.0) * 30000.0
    cbf = np.concatenate([ident, tri, 1.0 - tri, np.ones((128, 128), np.float32), negm], axis=1).astype(bf)
    maskh = ((j[:, None] // 64 == j[None, :] // 64) & (j[:, None] <= j[None, :])).astype(np.float32)
    cf = np.concatenate([maska, maskh], axis=1).astype(np.float32)
    return cbf, cf


def kernel(x_prompt, x_sample, cache_k, cache_v, state_s, w_in, w_out, lb_logits, hgrn_norm_g, ln_g, ln_b):
    f32 = np.float32
    x_prompt = np.asarray(x_prompt, f32); x_sample = np.asarray(x_sample, f32)
    cache_k = np.asarray(cache_k, f32); cache_v = np.asarray(cache_v, f32)
    state_s = np.asarray(state_s, f32)
    w_in0 = np.asarray(w_in, f32)[0]; w_out0 = np.asarray(w_out, f32)[0]
    cols = [np.arange(512, 1024), np.arange(1024, 1536), np.arange(3072, 3584)]
    for pr in range(4):
        cols += [np.arange(pr * 128, (pr + 1) * 128), np.arange(1536 + pr * 128, 1536 + (pr + 1) * 128)]
    for hd in range(4):
        cols += [np.arange(2048 + hd * 128, 2048 + (hd + 1) * 128), np.arange(2560 + hd * 128, 2560 + (hd + 1) * 128),
                 np.arange(3584 + hd * 128, 3584 + (hd + 1) * 128)]
    w_in_r = np.ascontiguousarray(w_in0[:, np.concatenate(cols)])
    lbl = np.ascontiguousarray(np.asarray(lb_logits, f32).reshape(2, 4, 128).transpose(2, 0, 1).reshape(128, 8))
    ng = np.ascontiguousarray(np.asarray(hgrn_norm_g, f32).reshape(4, 128).T)
    cbf, cf = _consts()
    in_maps = []
    for b in range(NCORES):
        xtok = np.concatenate([x_prompt[b], x_sample[b]], axis=0)
        in_maps.append({
            "xT": np.ascontiguousarray(xtok.T),
            "xtok": np.ascontiguousarray(xtok),
            "ckT": np.ascontiguousarray(cache_k[0, b].transpose(0, 2, 1).reshape(4, 128, PAST)),
            "cv": np.ascontiguousarray(cache_v[0, b].transpose(1, 0, 2).reshape(PAST, 512)),
            "state": np.ascontiguousarray(state_s[0, b]),
            "w_in": w_in_r, "w_out": w_out0, "lbl": lbl, "ng": ng,
            "ln_g": np.asarray(ln_g, f32).reshape(1, D), "ln_b": np.asarray(ln_b, f32).reshape(1, D),
            "cbf": cbf, "cf": cf,
        })
    if "nc" not in _NC_CACHE:
        _NC_CACHE["nc"] = build_program()
    res = run_bass_kernel_spmd(_NC_CACHE["nc"], in_maps, core_ids=list(range(NCORES)))
    r = res.results
    y = np.stack([r[b]["y"] for b in range(NCORES)])
    return (np.ascontiguousarray(y[:, :T]), np.ascontiguousarray(y[:, T:]),
            np.stack([r[b]["kp"] for b in range(NCORES)])[None],
            np.stack([r[b]["vp"] for b in range(NCORES)])[None],
            np.stack([r[b]["sp"] for b in range(NCORES)])[None],
            np.stack([r[b]["ks"] for b in range(NCORES)])[None],
            np.stack([r[b]["vs"] for b in range(NCORES)])[None],
            np.stack([r[b]["ss"] for b in range(NCORES)])[None])
```
